# Optimizing a Trainium2 kernel written in Bass

```python
import math
import jax, jax.numpy as jnp
from jax import lax
import numpy as np

D_MODEL = 1024
BATCH = 8
SEQ = 4096
DEPTH = 2

CHUNK = 64
N_EVEN = (DEPTH + 1) // 2
N_ODD = DEPTH // 2
N_SUB = 3
D_FF = ((8 * D_MODEL // 3 + 127) // 128) * 128
RMS_EPS = 1e-6

GM_BLOCK = 128
GM_WIDTH = D_MODEL
GM_GROUPS = 8
GM_GROUP_DIM = GM_WIDTH // GM_GROUPS

SSD_WIDTH = D_MODEL
SSD_HEAD_DIM = 64
SSD_HEADS = SSD_WIDTH // SSD_HEAD_DIM
SSD_GROUPS = 2
SSD_STATE = 128
SSD_CONV = 4
SSD_CHUNK = CHUNK
SSD_CONV_DIM = SSD_WIDTH + 2 * SSD_GROUPS * SSD_STATE

IN_WIDTH = 2 * GM_WIDTH + SSD_WIDTH + SSD_CONV_DIM + SSD_HEADS
MIX_WIDTH = GM_WIDTH + SSD_WIDTH

SB_WIDTH = D_MODEL
SB_HEAD_DIM = 64
SB_HEADS = SB_WIDTH // SB_HEAD_DIM
SB_BLOCK = 128

kernel_name = "hybrid_gmlp_ssd_stickbreaking_macaron_adaln"


def rms_norm(x, g):
    xf = x.astype(jnp.float32)
    y = xf * lax.rsqrt(jnp.mean(xf * xf, axis=-1, keepdims=True) + RMS_EPS)
    return (y * g.astype(jnp.float32)).astype(x.dtype)


def modulate(h, shift, scale):
    return h * (1 + scale[:, None, :]) + shift[:, None, :]


def swiglu(h, w_gate, w_up, w_down):
    return (jax.nn.silu(h @ w_gate) * (h @ w_up)) @ w_down


def chunked_gmlp(u, v, v_norm_g, w_s, b_s):
    b, s, _ = u.shape
    nc = s // GM_BLOCK
    v = v.reshape(b, nc, GM_BLOCK, GM_GROUPS, GM_GROUP_DIM)
    v = rms_norm(v, v_norm_g.reshape(GM_GROUPS, GM_GROUP_DIM))
    pos = np.arange(GM_BLOCK)
    mask = (pos[None, :] // CHUNK) <= (pos[:, None] // CHUNK)
    w = w_s * jnp.asarray(mask, dtype=w_s.dtype)[None]
    mixed = jnp.einsum('gts,bnsgc->bntgc', w, v) + b_s.T[:, :, None]
    return u * mixed.reshape(b, s, GM_WIDTH)


def causal_dwconv(x, w, bias):
    k = w.shape[0]
    s = x.shape[1]
    xp = jnp.pad(x, ((0, 0), (k - 1, 0), (0, 0)))
    return sum(xp[:, i:i + s] * w[i] for i in range(k)) + bias


def ssd_mixer(z, xbc, dt_raw, conv_w, conv_b, dt_bias, a_log, d_skip, norm_g):
    b, s, _ = z.shape
    nc = s // SSD_CHUNK
    e = SSD_HEADS // SSD_GROUPS
    xbc = jax.nn.silu(causal_dwconv(xbc, conv_w, conv_b))
    xs, bm, cm = jnp.split(xbc, [SSD_WIDTH, SSD_WIDTH + SSD_GROUPS * SSD_STATE], axis=-1)
    xs = xs.reshape(b, nc, SSD_CHUNK, SSD_GROUPS, e, SSD_HEAD_DIM)
    bm = bm.reshape(b, nc, SSD_CHUNK, SSD_GROUPS, SSD_STATE)
    cm = cm.reshape(b, nc, SSD_CHUNK, SSD_GROUPS, SSD_STATE)
    dt = jax.nn.softplus((dt_raw + dt_bias).astype(jnp.float32))
    dt = dt.reshape(b, nc, SSD_CHUNK, SSD_GROUPS, e)
    a = -jnp.exp(a_log.astype(jnp.float32)).reshape(SSD_GROUPS, e)
    cs = jnp.cumsum(dt * a, axis=2)
    seg = cs[:, :, :, None] - cs[:, :, None, :]
    pos = np.arange(SSD_CHUNK)
    causal = (pos[:, None] >= pos[None, :])[None, None, :, :, None, None]
    decay = jnp.exp(jnp.where(causal, seg, -jnp.inf))
    cb = jnp.einsum('bctgn,bcsgn->bctsg', cm, bm)
    w_ts = (cb[..., None] * decay * dt[:, :, None]).astype(xs.dtype)
    y_diag = jnp.einsum('bctsge,bcsgep->bctgep', w_ts, xs)
    decay_to_end = (jnp.exp(cs[:, :, -1:] - cs) * dt).astype(xs.dtype)
    states = jnp.einsum('bclgn,bclge,bclgep->bcgepn', bm, decay_to_end, xs)
    chunk_decay = jnp.exp(cs[:, :, -1]).astype(xs.dtype)

    def step(h, inp):
        st, dec = inp
        return h * dec[..., None, None] + st, h

    h0 = jnp.zeros_like(states[:, 0])
    _, h_in = lax.scan(step, h0, (jnp.moveaxis(states, 1, 0), jnp.moveaxis(chunk_decay, 1, 0)))
    h_in = jnp.moveaxis(h_in, 0, 1)
    y_off = jnp.einsum('bctgn,bcgepn->bctgep', cm, h_in) * jnp.exp(cs).astype(xs.dtype)[..., None]
    y = y_diag + y_off + d_skip.reshape(SSD_GROUPS, e)[:, :, None] * xs
    y = y.reshape(b, s, SSD_WIDTH)
    return rms_norm(y * jax.nn.silu(z), norm_g)


def hybrid_mixer(h, w_in, w_out, v_norm_g, w_s, b_s, conv_w, conv_b, dt_bias, a_log, d_skip, ssd_norm_g):
    proj = h @ w_in
    o1 = 2 * GM_WIDTH
    o2 = o1 + SSD_WIDTH
    o3 = o2 + SSD_CONV_DIM
    gm_part, z, xbc, dt_raw = jnp.split(proj, [o1, o2, o3], axis=-1)
    u, v = jnp.split(jax.nn.gelu(gm_part), 2, axis=-1)
    y_a = chunked_gmlp(u, v, v_norm_g, w_s, b_s)
    y_b = ssd_mixer(z, xbc, dt_raw, conv_w, conv_b, dt_bias, a_log, d_skip, ssd_norm_g)
    return jnp.concatenate([y_a, y_b], axis=-1) @ w_out


def stick_breaking_attention(q, k, v):
    b, nh, s, d = q.shape
    scale = d ** -0.5
    outs = []
    for i in range(s // SB_BLOCK):
        q0 = i * SB_BLOCK
        kv_len = q0 + SB_BLOCK
        qb = q[:, :, q0:kv_len]
        kb = k[:, :, :kv_len]
        vb = v[:, :, :kv_len]
        logits = jnp.einsum('bhtd,bhsd->bhts', qb, kb).astype(jnp.float32) * scale
        t_idx = q0 + np.arange(SB_BLOCK)[:, None]
        s_idx = np.arange(kv_len)[None, :]
        strict = jnp.asarray(s_idx < t_idx)
        log_beta = jax.nn.log_sigmoid(logits)
        log_keep = jnp.where(strict, jax.nn.log_sigmoid(-logits), 0.0)
        after = lax.cumsum(log_keep, axis=3, reverse=True) - log_keep
        wgt = jnp.where(strict, jnp.exp(log_beta + after), 0.0)
        outs.append(jnp.einsum('bhts,bhsd->bhtd', wgt.astype(vb.dtype), vb))
    return jnp.concatenate(outs, axis=2)


def sb_mixer(h, w_qkv, q_norm_g, k_norm_g, w_o):
    b, s, _ = h.shape
    qkv = (h @ w_qkv).reshape(b, s, 3, SB_HEADS, SB_HEAD_DIM)
    q = rms_norm(qkv[:, :, 0], q_norm_g).transpose(0, 2, 1, 3)
    k = rms_norm(qkv[:, :, 1], k_norm_g).transpose(0, 2, 1, 3)
    v = qkv[:, :, 2].transpose(0, 2, 1, 3)
    o = stick_breaking_attention(q, k, v)
    return o.transpose(0, 2, 1, 3).reshape(b, s, SB_WIDTH) @ w_o


def setup_inputs(seed: int = 0) -> dict:
    key = jax.random.key(seed)
    ks = iter(jax.random.split(key, 32))

    def nrm(shape, scale):
        return jax.random.normal(next(ks), shape, jnp.float32) * scale

    d = D_MODEL
    dt0 = jnp.exp(jax.random.uniform(next(ks), (N_EVEN, SSD_HEADS), jnp.float32,
                                     minval=math.log(1e-3), maxval=math.log(1e-1)))
    return {
        "x": nrm((BATCH, SEQ, d), 1.0),
        "c": nrm((BATCH, d), 1.0),
        "mod_w": nrm((DEPTH, d, N_SUB * 3 * d), 0.5 * d ** -0.5),
        "mod_b": nrm((DEPTH, N_SUB * 3 * d), 0.01),
        "norm_g": 1.0 + nrm((DEPTH, 4, d), 0.02),
        "ffn_w_gate": nrm((DEPTH, 2, d, D_FF), d ** -0.5),
        "ffn_w_up": nrm((DEPTH, 2, d, D_FF), d ** -0.5),
        "ffn_w_down": nrm((DEPTH, 2, D_FF, d), D_FF ** -0.5),
        "hy_w_in": nrm((N_EVEN, d, IN_WIDTH), d ** -0.5),
        "hy_w_out": nrm((N_EVEN, MIX_WIDTH, d), MIX_WIDTH ** -0.5),
        "gm_v_norm_g": 1.0 + nrm((N_EVEN, GM_WIDTH), 0.02),
        "gm_w_s": nrm((N_EVEN, GM_GROUPS, GM_BLOCK, GM_BLOCK), GM_BLOCK ** -0.5),
        "gm_b_s": 1.0 + nrm((N_EVEN, GM_GROUPS, GM_BLOCK), 0.02),
        "ssd_conv_w": nrm((N_EVEN, SSD_CONV, SSD_CONV_DIM), SSD_CONV ** -0.5),
        "ssd_conv_b": nrm((N_EVEN, SSD_CONV_DIM), 0.01),
        "ssd_dt_bias": dt0 + jnp.log(-jnp.expm1(-dt0)),
        "ssd_a_log": jnp.log(jax.random.uniform(next(ks), (N_EVEN, SSD_HEADS), jnp.float32,
                                                 minval=1.0, maxval=16.0)),
        "ssd_d": 1.0 + nrm((N_EVEN, SSD_HEADS), 0.1),
        "ssd_norm_g": 1.0 + nrm((N_EVEN, SSD_WIDTH), 0.02),
        "sb_w_qkv": nrm((N_ODD, d, 3 * SB_WIDTH), d ** -0.5),
        "sb_q_norm_g": 1.0 + nrm((N_ODD, SB_HEAD_DIM), 0.02),
        "sb_k_norm_g": 1.0 + nrm((N_ODD, SB_HEAD_DIM), 0.02),
        "sb_w_o": nrm((N_ODD, SB_WIDTH, d), SB_WIDTH ** -0.5),
    }


def reference(x, c, mod_w, mod_b, norm_g, ffn_w_gate, ffn_w_up, ffn_w_down,
              hy_w_in, hy_w_out, gm_v_norm_g, gm_w_s, gm_b_s,
              ssd_conv_w, ssd_conv_b, ssd_dt_bias, ssd_a_log, ssd_d, ssd_norm_g,
              sb_w_qkv, sb_q_norm_g, sb_k_norm_g, sb_w_o):
    cond = jax.nn.silu(c)
    h = x
    for layer in range(DEPTH):
        mod = (cond @ mod_w[layer] + mod_b[layer]).reshape(-1, N_SUB, 3, D_MODEL)
        y = modulate(rms_norm(h, norm_g[layer, 0]), mod[:, 0, 0], mod[:, 0, 1])
        h = h + 0.5 * mod[:, 0, 2][:, None] * swiglu(
            y, ffn_w_gate[layer, 0], ffn_w_up[layer, 0], ffn_w_down[layer, 0])
        y = modulate(rms_norm(h, norm_g[layer, 1]), mod[:, 1, 0], mod[:, 1, 1])
        j = layer // 2
        if layer % 2 == 0:
            mix = hybrid_mixer(y, hy_w_in[j], hy_w_out[j], gm_v_norm_g[j], gm_w_s[j], gm_b_s[j],
                               ssd_conv_w[j], ssd_conv_b[j], ssd_dt_bias[j], ssd_a_log[j],
                               ssd_d[j], ssd_norm_g[j])
        else:
            mix = sb_mixer(y, sb_w_qkv[j], sb_q_norm_g[j], sb_k_norm_g[j], sb_w_o[j])
        h = h + mod[:, 1, 2][:, None] * mix
        y = modulate(rms_norm(h, norm_g[layer, 2]), mod[:, 2, 0], mod[:, 2, 1])
        h = h + 0.5 * mod[:, 2, 2][:, None] * swiglu(
            y, ffn_w_gate[layer, 1], ffn_w_up[layer, 1], ffn_w_down[layer, 1])
        h = rms_norm(h, norm_g[layer, 3])
    return h
```

```python
import contextlib
import numpy as np
import concourse.bass as bass
import concourse.mybir as mybir
from concourse.bass_utils import run_bass_kernel_spmd

F32 = mybir.dt.float32
BF16 = mybir.dt.bfloat16
AF = mybir.ActivationFunctionType
ALU = mybir.AluOpType

D = 1024
S = 4096
KC = 8
T = 512
NT = S // T
DFF = 2816
NJ = DFF // 128
EPS = 1e-6
IN_W = 4624
NCORES = 8
SKIP = set()

COMPUTE = ("pe", "act", "dve", "pool")
STREAM_OF = {"pe": "pe", "act": "act", "dve": "dve", "pool": "pool", "sp": "sp", "pq": "pool", "aq": "act"}
DMAQ = ("sp", "pq", "aq")
NS_DMA = 8


class Op:
    __slots__ = ("eng", "fn", "deps", "signal", "sigval", "isdma", "dmak")

    def __init__(self, eng, fn, isdma):
        self.eng = eng
        self.fn = fn
        self.deps = []
        self.signal = False
        self.sigval = 0
        self.isdma = isdma
        self.dmak = -1


class Ctx:
    def __init__(self, nc, stack):
        self.nc = nc
        self.sems = {}
        for e in COMPUTE:
            self.sems[e] = stack.enter_context(nc.semaphore("s_" + e))
        for q in DMAQ:
            for i in range(NS_DMA):
                self.sems[(q, i)] = stack.enter_context(nc.semaphore("d_%s%d" % (q, i)))
        self.sigcount = {e: 0 for e in COMPUTE}
        self.dma_count = {q: 0 for q in DMAQ}
        self.dma_ops = {q: [] for q in DMAQ}
        self.uid = 0
        self.bar_sb = stack.enter_context(nc.sbuf_tensor("bar_sb", [128, 8], F32))
        self.bar_bf = stack.enter_context(nc.sbuf_tensor("bar_bf", [128, 8], BF16))

    def name(self, s):
        self.uid += 1
        return "%s_%d" % (s, self.uid)


class Prog:
    def __init__(self, C):
        self.C = C
        self.nc = C.nc
        self.streams = {e: [] for e in ("pe", "act", "dve", "pool", "sp")}
        self.res = {}
        self.all_ops = []

    def add(self, eng, fn, reads=(), writes=()):
        C = self.C
        isdma = eng in DMAQ
        op = Op(eng, fn, isdma)
        deps = []
        raw = []
        pr = [k for k in reads if (k if isinstance(k, str) else k[0]).startswith("ps")]
        if pr:
            writes = list(writes) + [k for k in pr if k not in writes]
        for k in reads:
            r = self.res.get(k)
            if r is not None:
                for lst in r[0].values():
                    raw.extend(lst)
        for k in writes:
            r = self.res.get(k)
            if r is not None:
                for lst in r[0].values():
                    deps.extend(lst)
                for lst in r[1].values():
                    deps.extend(lst)
        seen = set()
        for d in raw:
            if id(d) in seen:
                continue
            seen.add(id(d))
            if (not d.isdma) and (not isdma) and d.eng == eng and eng == "pe":
                continue
            op.deps.append(d)
        for d in deps:
            if id(d) in seen:
                continue
            seen.add(id(d))
            if (not d.isdma) and (not isdma) and d.eng == eng and eng == "pe":
                continue
            op.deps.append(d)
        if isdma:
            k = C.dma_count[eng]
            C.dma_count[eng] = k + 1
            op.dmak = k
            if k >= NS_DMA:
                op.deps.append(C.dma_ops[eng][k - NS_DMA])
            C.dma_ops[eng].append(op)
        for k in reads:
            r = self.res.setdefault(k, [{}, {}])
            if isdma:
                r[1].setdefault(eng, []).append(op)
            else:
                r[1][eng] = [op]
        for k in writes:
            self.res[k] = [{eng: [op]}, {}]
        self.streams[STREAM_OF[eng]].append(op)
        self.all_ops.append(op)
        return op

    def emit(self):
        C = self.C
        nc = self.nc
        sems = C.sems
        bar = C.bar_sb
        barb = C.bar_bf
        endops = {}
        endops["act"] = self.add("act", lambda e: e.memzero(bar[:, 0:1]), writes=["__bar_act"])
        endops["dve"] = self.add("dve", lambda e: e.memset(bar[:, 2:3], 0.0), writes=["__bar_dve"])
        endops["pool"] = self.add("pool", lambda e: e.memset(bar[:, 3:4], 0.0), writes=["__bar_pool"])
        for o in endops.values():
            o.signal = True
        for ops in self.streams.values():
            for op in ops:
                for d in op.deps:
                    d.signal = True
        pe_ops = self.streams["pe"]
        if pe_ops:
            pe_ops[-1].signal = True
        for e in COMPUTE:
            cnt = C.sigcount[e]
            for op in self.streams[e]:
                if op.isdma:
                    continue
                if op.signal:
                    cnt += 1
                    op.sigval = cnt
            C.sigcount[e] = cnt
        start_wait = dict(getattr(C, "bar_vals", {}))
        start_dma = dict(getattr(C, "bar_dma", {}))

        def tok(d):
            if d.isdma:
                return sems[(d.eng, d.dmak % NS_DMA)], 16 * (d.dmak // NS_DMA + 1)
            return sems[d.eng], d.sigval

        def run(stream, engobj):
            waited = {}
            for e, v in start_wait.items():
                if e != stream and v > 0:
                    engobj.wait_ge(sems[e], v)
                    waited[sems[e].name] = v
                elif e == stream:
                    waited[sems[e].name] = v
            for (q, i), v in start_dma.items():
                if v > 0:
                    engobj.wait_ge(sems[(q, i)], v)
                    waited[sems[(q, i)].name] = v
            for op in self.streams[stream]:
                for d in op.deps:
                    sem, val = tok(d)
                    if waited.get(sem.name, 0) >= val:
                        continue
                    waited[sem.name] = val
                    engobj.wait_ge(sem, val)
                ins = op.fn(engobj)
                if op.isdma:
                    ins.then_inc(sems[(op.eng, op.dmak % NS_DMA)], 16)
                elif op.signal:
                    ins.then_inc(sems[op.eng], 1)

        with nc.Block() as block:
            @block.sync
            def _(e):
                run("sp", e)

            @block.tensor
            def _(e):
                run("pe", e)

            @block.scalar
            def _(e):
                run("act", e)

            @block.vector
            def _(e):
                run("dve", e)

            @block.gpsimd
            def _(e):
                run("pool", e)
        C.bar_vals = {e: C.sigcount[e] for e in COMPUTE}
        bd = {}
        for q in DMAQ:
            n = C.dma_count[q]
            for i in range(NS_DMA):
                cnt = (n - i + NS_DMA - 1) // NS_DMA if n > i else 0
                bd[(q, i)] = 16 * cnt
        C.bar_dma = bd


def final_wait(C):
    nc = C.nc
    with nc.Block() as block:
        @block.sync
        def _(e):
            for (q, i), v in C.bar_dma.items():
                if v > 0:
                    e.wait_ge(C.sems[(q, i)], v)
            for en, v in C.bar_vals.items():
                if v > 0:
                    e.wait_ge(C.sems[en], v)


def sap(t, off, dims, p0=0, pn=128):
    row = 1
    for s in t.shape[1:]:
        row *= s
    return bass.AP(t, p0 * row + off, [[row, pn]] + [list(d) for d in dims])


class K:
    pass


def declare_io(nc, mode, need=None):
    g = K()
    g.names = []

    def di(n, s, dt=F32):
        if need is not None and n not in need:
            return None
        g.names.append(n)
        return nc.dram_tensor(n, list(s), dt, kind="ExternalInput")
    g.x = di("x", [S, D])
    g.hin = di("hin", [D, S]) if (need is not None and "hin" in need) else None
    g.cT = di("cT", [128, KC])
    g.mod_w = [di("mod_w%d" % l, [D, 9 * D]) for l in range(2)]
    g.mod_bT = di("mod_bT", [128, 2 * 72])
    g.norm_gT = di("norm_gT", [128, 2 * 4 * KC])
    g.wg = {(l, f): di("wg_%d%d" % (l, f), [D, DFF]) for l in range(2) for f in range(2)}
    g.wu = {(l, f): di("wu_%d%d" % (l, f), [D, DFF]) for l in range(2) for f in range(2)}
    g.wd = {(l, f): di("wd_%d%d" % (l, f), [DFF, D]) for l in range(2) for f in range(2)}
    g.hy_w_in = di("hy_w_in", [D, IN_W])
    g.hy_w_out = di("hy_w_out", [2 * D, D])
    g.gm_vg = di("gm_v_norm_g", [1, D])
    g.gm_ws = di("gm_w_s", [8, 128, 128])
    g.gm_bs = di("gm_b_s", [1, 8 * 128])
    g.conv_wT = di("conv_wT", [128, 12 * 4])
    g.conv_bT = di("conv_bT", [128, 12])
    g.dt_bias = di("ssd_dt_bias", [1, 16])
    g.a_log = di("ssd_a_log", [1, 16])
    g.ssd_d = di("ssd_d", [1, 16])
    g.ssd_ng = di("ssd_norm_g", [1, D])
    g.sb_wqkv = di("sb_w_qkv", [D, 3 * D])
    g.sb_qg = di("sb_qgT", [64, 1])
    g.sb_kg = di("sb_kgT", [64, 1])
    g.sb_wo = di("sb_w_o", [D, D])
    g.out = nc.dram_tensor("out", [S, D], F32, kind="ExternalOutput")
    ds = lambda n, s, dt: nc.dram_tensor(n, list(s), dt, kind="Internal")
    if mode in ("full", "full_test"):
        g.hT = [ds("hT0", [D, S], F32), ds("hT1", [D, S], F32)]
    else:
        g.hT = [nc.dram_tensor("hT0", [D, S], F32, kind="ExternalOutput"), ds("hT1", [D, S], F32)]
        g.dbg = nc.dram_tensor("dbg", [128, 2048], F32, kind="ExternalOutput")
        g.dbgb = nc.dram_tensor("dbgb", [128, 16 * T], BF16, kind="ExternalOutput")
    has = lambda l, f: g.wg[(l, f)] is not None
    g.wg_bf = {(l, f): ds("wg_bf%d%d" % (l, f), [D, DFF], BF16) for l in range(2) for f in range(2) if has(l, f)}
    g.wu_bf = {(l, f): ds("wu_bf%d%d" % (l, f), [D, DFF], BF16) for l in range(2) for f in range(2) if has(l, f)}
    g.wd_bf = {(l, f): ds("wd_bf%d%d" % (l, f), [DFF, D], BF16) for l in range(2) for f in range(2) if has(l, f)}
    g.hy_w_in_bf = ds("hy_w_in_bf", [D, IN_W], BF16)
    g.hy_w_out_bf = ds("hy_w_out_bf", [2 * D, D], BF16)
    g.sb_wqkv_bf = ds("sb_wqkv_bf", [D, 3 * D], BF16)
    g.sb_wo_bf = ds("sb_wo_bf", [D, D], BF16)
    g.qT = ds("qT_s", [16, 64, S], BF16)
    g.kT = ds("kT_s", [16, 64, S], BF16)
    g.v_s = ds("v_s", [S, D], BF16)
    g.o_s = ds("o_s", [S, D], BF16)
    return g


def dap(t, off, dims):
    return bass.AP(t, off, [list(d) for d in dims])


def emit_casts(p, g, items, collect=None):
    def cast2d(src, dst, rows, cols):
        for r0 in range(0, rows, 128):
            fn = (lambda e, r0=r0: e.dma_start(
                out=dap(dst, r0 * cols, [[cols, 128], [1, cols]]),
                in_=dap(src, r0 * cols, [[cols, 128], [1, cols]])))
            if collect is None:
                p.add("pq", fn, writes=[])
            else:
                collect.append(fn)
    for it in items:
        if it == "ffn":
            for (l, f) in sorted(g.wg_bf.keys()):
                cast2d(g.wg[(l, f)], g.wg_bf[(l, f)], D, DFF)
                cast2d(g.wu[(l, f)], g.wu_bf[(l, f)], D, DFF)
                cast2d(g.wd[(l, f)], g.wd_bf[(l, f)], DFF, D)
        elif isinstance(it, tuple):
            l, f = it
            cast2d(g.wg[(l, f)], g.wg_bf[(l, f)], D, DFF)
            cast2d(g.wu[(l, f)], g.wu_bf[(l, f)], D, DFF)
            cast2d(g.wd[(l, f)], g.wd_bf[(l, f)], DFF, D)
        elif it == "hy":
            cast2d(g.hy_w_in, g.hy_w_in_bf, D, IN_W)
            cast2d(g.hy_w_out, g.hy_w_out_bf, 2 * D, D)
        elif it == "sb":
            cast2d(g.sb_wqkv, g.sb_wqkv_bf, D, 3 * D)
            cast2d(g.sb_wo, g.sb_wo_bf, D, D)


def setup_pass(C, g, cs, layers=(0, 1), cast=("ffn", "hy", "sb")):
    nc = C.nc
    p = Prog(C)
    with contextlib.ExitStack() as es:
        sb = lambda n, s, dt: es.enter_context(nc.sbuf_tensor(C.name(n), list(s), dt))
        emit_casts(p, g, cast)

        ident = cs.ident
        p.add("pool", lambda e: e.memset(ident[:], 1.0), writes=["ident"])
        p.add("pool", lambda e: e.affine_select(out=ident[:], in_=ident[:], pattern=[[-1, 128]], compare_op=ALU.is_equal,
                                                fill=0.0, base=0, channel_multiplier=1), reads=["ident"], writes=["ident"])
        p.add("dve", lambda e: e.tensor_copy(out=cs.ident_bf[:], in_=ident[:]), reads=["ident"], writes=["ident_bf"])
        p.add("dve", lambda e: e.memset(cs.ones_bf[:], 1.0), writes=["ones_bf"])
        p.add("sp", lambda e: e.dma_start(out=cs.condT[:], in_=g.cT.ap()), writes=["condT"])
        p.add("sp", lambda e: e.dma_start(out=cs.modT[:], in_=g.mod_bT.ap()), writes=["mod_b"])
        p.add("sp", lambda e: e.dma_start(out=cs.normg[:], in_=g.norm_gT.ap()), writes=["normg"])
        p.add("act", lambda e: e.activation(out=cs.condT[:], in_=cs.condT[:], func=AF.Silu), reads=["condT"], writes=["condT"])
        wbuf = [sb("modw", [128, KC, 512], F32) for _ in range(2)]
        psm = es.enter_context(nc.psum_tensor(C.name("psm"), [128, 512], F32))
        modb = sb("modb", [128, 144], F32)
        p.add("dve", lambda e: e.tensor_copy(out=modb[:], in_=cs.modT[:]), reads=["mod_b"], writes=["modb"])
        it = 0
        for l in layers:
            for cg in range(18):
                wb = wbuf[it % 2]
                key = "modw%d" % (it % 2)
                it += 1
                p.add("sp", lambda e, l=l, cg=cg, wb=wb: e.dma_start(
                    out=wb[:], in_=dap(g.mod_w[l], cg * 512, [[9 * D, 128], [128 * 9 * D, KC], [1, 512]])),
                    writes=[key])
                for c4 in range(4):
                    oc = cg * 4 + c4
                    col = l * 72 + oc
                    for kc in range(KC):
                        p.add("pe", lambda e, wb=wb, c4=c4, kc=kc, col=col: e.matmul(
                            psm[:, col:col + 1], lhsT=wb[:, kc, c4 * 128:(c4 + 1) * 128], rhs=cs.condT[:, kc:kc + 1],
                            start=(kc == 0), stop=(kc == KC - 1)), reads=[key, "condT"], writes=["psm"])
        for l in layers:
            p.add("dve", lambda e, l=l: e.tensor_tensor(out=cs.modT[:, l * 72:(l + 1) * 72], in0=psm[:, l * 72:(l + 1) * 72],
                                                        in1=modb[:, l * 72:(l + 1) * 72], op=ALU.add),
                  reads=["psm", "modb"], writes=["modT"])
        for l in layers:
            for sub in range(3):
                sc = cs.modT[:, l * 72 + sub * 24 + 8: l * 72 + sub * 24 + 16]
                ng = cs.normg[:, (l * 4 + sub) * KC:(l * 4 + sub + 1) * KC]
                a_out = cs.A[:, (l * 3 + sub) * KC:(l * 3 + sub + 1) * KC]
                p.add("dve", lambda e, sc=sc, ng=ng, a_out=a_out: e.scalar_tensor_tensor(
                    out=a_out, in0=sc, scalar=1.0, in1=ng, op0=ALU.add, op1=ALU.mult),
                    reads=["modT", "normg"], writes=["A"])
                gt = cs.modT[:, l * 72 + sub * 24 + 16: l * 72 + sub * 24 + 24]
                g_out = cs.gate[:, (l * 3 + sub) * KC:(l * 3 + sub + 1) * KC]
                p.add("dve", lambda e, gt=gt, g_out=g_out, sub=sub: e.tensor_scalar(
                    out=g_out, in0=gt, scalar1=(1.0 if sub == 1 else 0.5), scalar2=None, op0=ALU.mult),
                    reads=["modT"], writes=["gate"])
        p.emit()


class Consts:
    def __init__(self, C, stack):
        nc = C.nc
        sb = lambda n, s, dt: stack.enter_context(nc.sbuf_tensor(n, list(s), dt))
        self.ident = sb("ident", [128, 128], F32)
        self.ident_bf = sb("ident_bf", [128, 128], BF16)
        self.ones_bf = sb("ones_bf", [128, 128], BF16)
        self.condT = sb("condT", [128, KC], F32)
        self.modT = sb("modT", [128, 144], F32)
        self.normg = sb("normg", [128, 64], F32)
        self.A = sb("Amod", [128, 6 * KC], F32)
        self.gate = sb("gate", [128, 6 * KC], F32)

    def shift(self, l, sub):
        return self.modT[:, l * 72 + sub * 24: l * 72 + sub * 24 + 8]


class TileIO:
    def __init__(self, C, p, es, g, cs, src_kind, src, dst, l, sub, prenorm=None, nbuf=2):
        nc = C.nc
        self.C, self.p, self.g, self.cs = C, p, g, cs
        self.src_kind, self.src, self.dst = src_kind, src, dst
        self.l, self.sub, self.prenorm = l, sub, prenorm
        sb = lambda n, s, dt: es.enter_context(nc.sbuf_tensor(C.name(n), list(s), dt))
        ps = lambda n: es.enter_context(nc.psum_tensor(C.name(n), [128, 512], F32))
        self.nbuf = nbuf
        self.hT = [sb("hTt", [128, KC, T], F32) for _ in range(nbuf)]
        self.yT = [sb("yT", [128, KC, T], BF16) for _ in range(nbuf)]
        self.sq = sb("sq", [128, KC, T], BF16)
        self.rstd = sb("rstd", [128, T], F32)
        self.tmp = [sb("ntmp", [128, T], F32) for _ in range(2)]
        self.psn = ps("psn")
        if src_kind == "x":
            self.xin = sb("xin", [128, 4, D], F32)
            self.psT = ps("psT")

    def load(self, i):
        p, g = self.p, self.g
        b = i % self.nbuf
        hT = self.hT[b]
        if self.src_kind == "hT":
            p.add("sp", lambda e: e.dma_start(out=hT[:], in_=dap(self.src, i * T, [[S, 128], [128 * S, KC], [1, T]])),
                  writes=[("hT", b, kc) for kc in range(KC)])
        else:
            xin = self.xin
            p.add("sp", lambda e: e.dma_start(out=xin[:], in_=dap(self.src, i * T * D, [[D, 128], [128 * D, 4], [1, D]])),
                  writes=["xin"])
            for kc in range(KC):
                for s4 in range(4):
                    p.add("pe", lambda e, kc=kc, s4=s4: e.transpose(self.psT[:, s4 * 128:(s4 + 1) * 128],
                                                                      xin[:, s4, kc * 128:(kc + 1) * 128], self.cs.ident[:]),
                          reads=["xin", "ident"], writes=["psT"])
                if kc % 2 == 0:
                    p.add("act", lambda e, kc=kc: e.activation(out=hT[:, kc, :], in_=self.psT[:], func=AF.Copy),
                          reads=["psT"], writes=[("hT", b, kc)])
                else:
                    p.add("dve", lambda e, kc=kc: e.tensor_copy(out=hT[:, kc, :], in_=self.psT[:]),
                          reads=["psT"], writes=[("hT", b, kc)])

    def _rstd(self, b):
        p = self.p
        hT = self.hT[b]
        p.add("act", lambda e: e.activation(out=self.sq[:], in_=hT[:], func=AF.Square),
              reads=[("hT", b, kc) for kc in range(KC)], writes=["sq"])
        for kc in range(KC):
            p.add("pe", lambda e, kc=kc: e.matmul(self.psn[:], lhsT=self.cs.ones_bf[:], rhs=self.sq[:, kc, :],
                                                  start=(kc == 0), stop=(kc == KC - 1)),
                  reads=["sq", "ones_bf"], writes=["psn"])
        p.add("act", lambda e: e.activation(out=self.rstd[:], in_=self.psn[:], func=AF.Ln, scale=1.0 / D, bias=EPS),
              reads=["psn"], writes=["rstd"])
        p.add("act", lambda e: e.activation(out=self.rstd[:], in_=self.rstd[:], func=AF.Exp, scale=-0.5),
              reads=["rstd"], writes=["rstd"])

    def norm(self, i):
        p, cs = self.p, self.cs
        b = i % self.nbuf
        hT, yT = self.hT[b], self.yT[b]
        if self.prenorm is not None:
            self._rstd(b)
            pl = self.prenorm
            for kc in range(KC):
                gcol = cs.normg[:, (pl * 4 + 3) * KC + kc:(pl * 4 + 3) * KC + kc + 1]
                p.add("dve", lambda e, kc=kc, gcol=gcol: e.scalar_tensor_tensor(
                    out=hT[:, kc, :], in0=hT[:, kc, :], scalar=gcol, in1=self.rstd[:], op0=ALU.mult, op1=ALU.mult),
                    reads=[("hT", b, kc), "rstd", "normg"], writes=[("hT", b, kc)])
        self._rstd(b)
        l, sub = self.l, self.sub
        for kc in range(KC):
            acol = cs.A[:, (l * 3 + sub) * KC + kc:(l * 3 + sub) * KC + kc + 1]
            scol = cs.modT[:, l * 72 + sub * 24 + kc: l * 72 + sub * 24 + kc + 1]
            tmp = self.tmp[kc % 2]
            tk = "ntmp%d" % (kc % 2)
            p.add("dve", lambda e, kc=kc, acol=acol, tmp=tmp: e.scalar_tensor_tensor(
                out=tmp[:], in0=hT[:, kc, :], scalar=acol, in1=self.rstd[:], op0=ALU.mult, op1=ALU.mult),
                reads=[("hT", b, kc), "rstd", "A"], writes=[tk])
            p.add("act", lambda e, kc=kc, scol=scol, tmp=tmp: e.activation(
                out=yT[:, kc, :], in_=tmp[:], func=AF.Identity, bias=scol),
                reads=[tk, "modT"], writes=[("yT", b, kc)])

    def residual(self, i, dc, pso, pso_key):
        p, cs = self.p, self.cs
        b = i % self.nbuf
        hT = self.hT[b]
        l, sub = self.l, self.sub
        gcol = cs.gate[:, (l * 3 + sub) * KC + dc:(l * 3 + sub) * KC + dc + 1]
        p.add("dve", lambda e: e.scalar_tensor_tensor(out=hT[:, dc, :], in0=pso[:], scalar=gcol, in1=hT[:, dc, :],
                                                      op0=ALU.mult, op1=ALU.add),
              reads=[pso_key, ("hT", b, dc), "gate"], writes=[("hT", b, dc)])

    def store(self, i):
        p = self.p
        b = i % self.nbuf
        hT = self.hT[b]
        p.add("aq", lambda e: e.dma_start(out=dap(self.dst, i * T, [[S, 128], [128 * S, KC], [1, T]]), in_=hT[:]),
              reads=[("hT", b, kc) for kc in range(KC)])


def ffn_pass(C, g, cs, l, f, src_kind, src, dst, prenorm=None, ntiles=NT, extra_casts=()):
    nc = C.nc
    sub = 0 if f == 0 else 2
    NH = 2 if ntiles % 2 == 0 else 1
    p = Prog(C)
    with contextlib.ExitStack() as es:
        sb = lambda n, s, dt: es.enter_context(nc.sbuf_tensor(C.name(n), list(s), dt))
        ps = lambda n: es.enter_context(nc.psum_tensor(C.name(n), [128, 512], F32))
        io = TileIO(C, p, es, g, cs, src_kind, src, dst, l, sub, prenorm, nbuf=NH)
        a = sb("a", [128, NJ, NH, T], BF16)
        sg = [sb("sg", [128, T], F32) for _ in range(2)]
        wgb = [sb("wgb", [128, KC, 512], BF16) for _ in range(2)]
        wub = [sb("wub", [128, KC, 512], BF16) for _ in range(2)]
        wdb = [sb("wdb", [128, NJ, 512], BF16) for _ in range(2)]
        psg = [ps("psg") for _ in range(2)]
        psu = [ps("psu") for _ in range(2)]
        pso = [ps("pso") for _ in range(2)]
        wg_bf, wu_bf, wd_bf = g.wg_bf[(l, f)], g.wu_bf[(l, f)], g.wd_bf[(l, f)]
        groups = [(0, 4), (4, 4), (8, 4), (12, 4), (16, 4), (20, 2)]
        tasks = []
        cnt = {"g": 0, "h": 0, "j": 0, "o": 0}

        def mk_group(st, gi):
            j0, nj = groups[gi]
            par = cnt["g"] % 2
            cnt["g"] += 1
            ncols = nj * 128

            def load():
                p.add("sp", lambda e: e.dma_start(out=wgb[par][:, :, 0:ncols],
                                                  in_=dap(wg_bf, j0 * 128, [[DFF, 128], [128 * DFF, KC], [1, ncols]])),
                      writes=[("wgb", par)])
                p.add("sp", lambda e: e.dma_start(out=wub[par][:, :, 0:ncols],
                                                  in_=dap(wu_bf, j0 * 128, [[DFF, 128], [128 * DFF, KC], [1, ncols]])),
                      writes=[("wub", par)])

            def compute():
                for jj in range(nj):
                    j = j0 + jj
                    for hf in range(NH):
                        yT = io.yT[hf]
                        pj = cnt["j"] % 2
                        cnt["j"] += 1
                        for kc in range(KC):
                            p.add("pe", lambda e, kc=kc, jj=jj, pj=pj, yT=yT: e.matmul(
                                psg[pj][:], lhsT=wgb[par][:, kc, jj * 128:(jj + 1) * 128], rhs=yT[:, kc, :],
                                start=(kc == 0), stop=(kc == KC - 1)),
                                reads=[("wgb", par), ("yT", hf, kc)], writes=[("psg", pj)])
                        for kc in range(KC):
                            p.add("pe", lambda e, kc=kc, jj=jj, pj=pj, yT=yT: e.matmul(
                                psu[pj][:], lhsT=wub[par][:, kc, jj * 128:(jj + 1) * 128], rhs=yT[:, kc, :],
                                start=(kc == 0), stop=(kc == KC - 1)),
                                reads=[("wub", par), ("yT", hf, kc)], writes=[("psu", pj)])
                        p.add("act", lambda e, pj=pj: e.activation(out=sg[pj][:], in_=psg[pj][:], func=AF.Silu),
                              reads=[("psg", pj)], writes=[("sg", pj)])
                        p.add("dve", lambda e, pj=pj, j=j, hf=hf: e.tensor_tensor(out=a[:, j, hf, :], in0=sg[pj][:], in1=psu[pj][:], op=ALU.mult),
                              reads=[("sg", pj), ("psu", pj)], writes=[("a", j, hf)])
            return load, compute

        def mk_half(st, half):
            par = cnt["h"] % 2
            cnt["h"] += 1

            def load():
                p.add("sp", lambda e: e.dma_start(out=wdb[par][:],
                                                  in_=dap(wd_bf, half * 512, [[D, 128], [128 * D, NJ], [1, 512]])),
                      writes=[("wdb", par)])

            def compute():
                for dcl in range(4):
                    dc = half * 4 + dcl
                    for hf in range(NH):
                        po = cnt["o"] % 2
                        cnt["o"] += 1
                        for j in range(NJ):
                            p.add("pe", lambda e, j=j, dcl=dcl, po=po, hf=hf: e.matmul(
                                pso[po][:], lhsT=wdb[par][:, j, dcl * 128:(dcl + 1) * 128], rhs=a[:, j, hf, :],
                                start=(j == 0), stop=(j == NJ - 1)),
                                reads=[("wdb", par), ("a", j, hf)], writes=[("pso", po)])
                        io.residual(st * NH + hf, dc, pso[po], ("pso", po))
                if half == 1:
                    for hf in range(NH):
                        io.store(st * NH + hf)
            return load, compute

        emit_casts(p, g, list(extra_casts))
        nst = ntiles // NH
        for st in range(nst):
            for gi in range(len(groups)):
                tasks.append(("g", st, gi) + mk_group(st, gi))
            for half in range(2):
                tasks.append(("h", st, half) + mk_half(st, half))

        tasks[0][3]()
        for k, (kind, st, idx, load, compute) in enumerate(tasks):
            if kind == "g" and idx == 0:
                for hf in range(NH):
                    io.load(st * NH + hf)
                for hf in range(NH):
                    io.norm(st * NH + hf)
            if k + 1 < len(tasks):
                tasks[k + 1][3]()
            compute()
        p.emit()


def attn_consts(C, p, es, g):
    nc = C.nc
    sb = lambda n, s, dt: es.enter_context(nc.sbuf_tensor(C.name(n), list(s), dt))
    k = K()
    tmp = sb("actmp", [128, 128], F32)
    k.blockones = sb("blockones", [128, 128], BF16)
    k.negtri = sb("negtri", [128, 128], BF16)
    k.maskst = sb("maskst", [128, 128], BF16)
    k.negones = sb("negones", [128, 2], BF16)
    p.add("pool", lambda e: e.memset(tmp[:], 0.0), writes=["actmp"])
    p.add("pool", lambda e: e.memset(tmp[0:64, 0:64], 1.0), writes=["actmp"])
    p.add("pool", lambda e: e.memset(tmp[64:128, 64:128], 1.0), writes=["actmp"])
    p.add("dve", lambda e: e.tensor_copy(out=k.blockones[:], in_=tmp[:]), reads=["actmp"], writes=["blockones"])
    p.add("pool", lambda e: e.memset(tmp[:], -1.0), writes=["actmp"])
    p.add("pool", lambda e: e.affine_select(out=tmp[:], in_=tmp[:], pattern=[[-1, 128]], compare_op=ALU.is_ge, fill=0.0,
                                            base=0, channel_multiplier=1), reads=["actmp"], writes=["actmp"])
    p.add("dve", lambda e: e.tensor_copy(out=k.negtri[:], in_=tmp[:]), reads=["actmp"], writes=["negtri"])
    p.add("pool", lambda e: e.memset(tmp[:], 1.0), writes=["actmp"])
    p.add("pool", lambda e: e.affine_select(out=tmp[:], in_=tmp[:], pattern=[[1, 128]], compare_op=ALU.is_gt, fill=0.0,
                                            base=0, channel_multiplier=-1), reads=["actmp"], writes=["actmp"])
    p.add("dve", lambda e: e.tensor_copy(out=k.maskst[:], in_=tmp[:]), reads=["actmp"], writes=["maskst"])
    p.add("dve", lambda e: e.memset(k.negones[:], -1.0), writes=["negones"])
    return k


def attn_qkv_pass(C, g, cs, l, src, ntiles=NT):
    nc = C.nc
    p = Prog(C)
    with contextlib.ExitStack() as es:
        sb = lambda n, s, dt: es.enter_context(nc.sbuf_tensor(C.name(n), list(s), dt))
        ps = lambda n: es.enter_context(nc.psum_tensor(C.name(n), [128, 512], F32))
        io = TileIO(C, p, es, g, cs, "hT", src, None, l, 1)
        ac = attn_consts(C, p, es, g)
        w = sb("wqkv", [128, KC, 3 * D], BF16)
        gqk = sb("gqk", [128, 2], F32)
        sq2 = [sb("sq2", [128, T], BF16) for _ in range(2)]
        r2 = [sb("r2", [128, T], F32) for _ in range(2)]
        stg = [sb("stg", [128, T], BF16) for _ in range(2)]
        vst = [sb("vst", [128, 4, D], BF16) for _ in range(2)]
        psqk = [ps("psqk") for _ in range(2)]
        psn2 = [ps("psn2") for _ in range(2)]
        psv = [ps("psv") for _ in range(2)]
        for kc in range(KC):
            p.add("sp", lambda e, kc=kc: e.dma_start(out=w[:, kc, :], in_=dap(g.sb_wqkv_bf, kc * 128 * 3 * D, [[3 * D, 128], [1, 3 * D]])),
                  writes=[("w", kc)])
        for half in range(2):
            p.add("sp", lambda e, half=half: e.dma_start(out=gqk[half * 64:(half + 1) * 64, 0:1], in_=g.sb_qg.ap()), writes=["gqk"])
            p.add("sp", lambda e, half=half: e.dma_start(out=gqk[half * 64:(half + 1) * 64, 1:2], in_=g.sb_kg.ap()), writes=["gqk"])
        p.add("dve", lambda e: e.tensor_scalar(out=gqk[:, 0:1], in0=gqk[:, 0:1], scalar1=0.125, scalar2=None, op0=ALU.mult),
              reads=["gqk"], writes=["gqk"])
        cnt = [0, 0]

        def qk_unit(i, hp, which):
            b = i % io.nbuf
            yT = io.yT[b]
            u = cnt[0]
            cnt[0] += 1
            par = u % 2
            col0 = which * D + hp * 128
            for kc in range(KC):
                p.add("pe", lambda e, kc=kc: e.matmul(psqk[par][:], lhsT=w[:, kc, col0:col0 + 128], rhs=yT[:, kc, :],
                                                      start=(kc == 0), stop=(kc == KC - 1)),
                      reads=[("w", kc), ("yT", b, kc)], writes=[("psqk", par)])
            p.add("act", lambda e: e.activation(out=sq2[par][:], in_=psqk[par][:], func=AF.Square),
                  reads=[("psqk", par)], writes=[("sq2", par)])
            p.add("pe", lambda e: e.matmul(psn2[par][:], lhsT=ac.blockones[:], rhs=sq2[par][:], start=True, stop=True),
                  reads=[("sq2", par), "blockones"], writes=[("psn2", par)])
            p.add("act", lambda e: e.activation(out=r2[par][:], in_=psn2[par][:], func=AF.Ln, scale=1.0 / 64, bias=EPS),
                  reads=[("psn2", par)], writes=[("r2", par)])
            p.add("act", lambda e: e.activation(out=r2[par][:], in_=r2[par][:], func=AF.Exp, scale=-0.5),
                  reads=[("r2", par)], writes=[("r2", par)])
            p.add("dve", lambda e: e.scalar_tensor_tensor(out=stg[par][:], in0=psqk[par][:], scalar=gqk[:, which:which + 1],
                                                          in1=r2[par][:], op0=ALU.mult, op1=ALU.mult),
                  reads=[("psqk", par), ("r2", par), "gqk"], writes=[("stg", par)])
            dst = g.qT if which == 0 else g.kT
            p.add("aq", lambda e: e.dma_start(out=dap(dst, hp * 128 * S + i * T, [[S, 128], [1, T]]), in_=stg[par][:]),
                  reads=[("stg", par)])

        def v_units(i):
            b = i % io.nbuf
            yT = io.yT[b]
            vt = vst[i % 2]
            for s4 in range(4):
                for half in range(2):
                    u = cnt[1]
                    cnt[1] += 1
                    par = u % 2
                    for kc in range(KC):
                        p.add("pe", lambda e, kc=kc, s4=s4, half=half, par=par: e.matmul(
                            psv[par][:], lhsT=yT[:, kc, s4 * 128:(s4 + 1) * 128],
                            rhs=w[:, kc, 2 * D + half * 512:2 * D + (half + 1) * 512], start=(kc == 0), stop=(kc == KC - 1)),
                            reads=[("w", kc), ("yT", b, kc)], writes=[("psv", par)])
                    if u % 2 == 0:
                        p.add("act", lambda e, s4=s4, half=half, par=par: e.activation(
                            out=vt[:, s4, half * 512:(half + 1) * 512], in_=psv[par][:], func=AF.Copy),
                            reads=[("psv", par)], writes=[("vst", i % 2, s4, half)])
                    else:
                        p.add("dve", lambda e, s4=s4, half=half, par=par: e.tensor_copy(
                            out=vt[:, s4, half * 512:(half + 1) * 512], in_=psv[par][:]),
                            reads=[("psv", par)], writes=[("vst", i % 2, s4, half)])
            p.add("aq", lambda e: e.dma_start(out=dap(g.v_s, i * T * D, [[D, 128], [128 * D, 4], [1, D]]), in_=vt[:]),
                  reads=[("vst", i % 2, s4, half) for s4 in range(4) for half in range(2)])

        io.load(0)
        io.norm(0)
        for i in range(ntiles):
            if i + 1 < ntiles:
                io.load(i + 1)
            for hp in range(8):
                qk_unit(i, hp, 0)
                qk_unit(i, hp, 1)
                if hp == 3 and i + 1 < ntiles:
                    io.norm(i + 1)
            v_units(i)
        p.emit()


def attn_core_pass(C, g, cs, ntiles=NT):
    nc = C.nc
    p = Prog(C)
    Se = ntiles * T
    nkb_tot = Se // 128
    with contextlib.ExitStack() as es:
        sb = lambda n, s, dt: es.enter_context(nc.sbuf_tensor(C.name(n), list(s), dt))
        ps = lambda n: es.enter_context(nc.psum_tensor(C.name(n), [128, 512], F32))
        ac = attn_consts(C, p, es, g)
        qt = [sb("qt", [128, Se], BF16) for _ in range(2)]
        kt = [sb("kt", [128, Se], BF16) for _ in range(2)]
        vv = [sb("vv", [128, nkb_tot, 128], BF16) for _ in range(2)]
        e1 = [sb("e1", [128, T], F32) for _ in range(2)]
        spm = [sb("spm", [128, T], BF16) for _ in range(2)]
        wl = [sb("wl", [128, T], BF16) for _ in range(2)]
        gg = [sb("gg", [128, 4], F32) for _ in range(2)]
        acc = [sb("acc", [128, 4, 64], F32) for _ in range(2)]
        oc = [sb("oc", [128, 4, 128], BF16) for _ in range(2)]
        psz = [ps("psz") for _ in range(2)]
        pse = [ps("pse") for _ in range(2)]
        psp = [ps("psp") for _ in range(2)]
        cnt = [0, 0]

        def load_pair(hp):
            b = hp % 2
            p.add("sp", lambda e: e.dma_start(out=qt[b][:], in_=dap(g.qT, hp * 128 * S, [[S, 128], [1, Se]])), writes=[("qt", b)])
            p.add("sp", lambda e: e.dma_start(out=kt[b][:], in_=dap(g.kT, hp * 128 * S, [[S, 128], [1, Se]])), writes=[("kt", b)])
            p.add("sp", lambda e: e.dma_start(out=vv[b][:], in_=dap(g.v_s, hp * 128, [[D, 128], [128 * D, nkb_tot], [1, 128]])),
                  writes=[("vv", b)])

        spm3 = spm + [sb("spm", [128, T], BF16)]

        class St:
            pass

        def mk(hp, hh, qc, kb, ab, u, last):
            st = St()
            st.hp, st.hh, st.qc, st.kb, st.ab, st.u, st.last = hp, hh, qc, kb, ab, u, last
            st.b = hp % 2
            st.pb = hh * 64
            r = kb - 4 * qc
            st.qb0 = max(r, 0)
            st.c0 = st.qb0 * 128
            st.diag = r >= 0
            st.ksl = sap(kt[st.b], kb * 128, [[1, 128]], p0=st.pb, pn=64)
            st.qsl = sap(qt[st.b], qc * T + st.c0, [[1, T - st.c0]], p0=st.pb, pn=64)
            return st

        def stageA(st):
            b, c0, u = st.b, st.c0, st.u
            par, p3 = u % 2, u % 3
            Z = psz[par]
            p.add("pe", lambda e: e.matmul(Z[:, c0:T], lhsT=st.ksl, rhs=st.qsl, start=True, stop=True),
                  reads=[("kt", b), ("qt", b)], writes=[("psz", par)])
            p.add("act", lambda e: e.activation(out=e1[par][:, c0:T], in_=Z[:, c0:T], func=AF.Exp),
                  reads=[("psz", par)], writes=[("e1", par)])
            p.add("act", lambda e: e.activation(out=spm3[p3][:, c0:T], in_=e1[par][:, c0:T], func=AF.Ln, bias=1.0),
                  reads=[("e1", par)], writes=[("spm", p3)])
            if st.diag:
                p.add("dve", lambda e: e.tensor_tensor(out=spm3[p3][:, c0:c0 + 128], in0=spm3[p3][:, c0:c0 + 128],
                                                        in1=ac.maskst[:], op=ALU.mult),
                      reads=[("spm", p3), "maskst"], writes=[("spm", p3)])

        def stageB1(st):
            b, c0, u = st.b, st.c0, st.u
            par, p3 = u % 2, u % 3
            E = pse[par]
            p.add("pe", lambda e: e.matmul(E[:, c0:T], lhsT=st.ksl, rhs=st.qsl, start=True, stop=False),
                  reads=[("kt", b), ("qt", b)], writes=[("pse", par)])
            p.add("pe", lambda e: e.matmul(E[:, c0:T], lhsT=ac.negtri[:], rhs=spm3[p3][:, c0:T], start=False, stop=True),
                  reads=[("spm", p3), "negtri"], writes=[("pse", par)])
            p.add("act", lambda e: e.activation(out=wl[par][:, c0:T], in_=E[:, c0:T], func=AF.Exp),
                  reads=[("pse", par)], writes=[("wl", par)])
            if st.diag:
                p.add("dve", lambda e: e.tensor_tensor(out=wl[par][:, c0:c0 + 128], in0=wl[par][:, c0:c0 + 128],
                                                        in1=ac.maskst[:], op=ALU.mult),
                      reads=[("wl", par), "maskst"], writes=[("wl", par)])

        def stageB2(st):
            b, c0, u, kb, hh, ab, qb0 = st.b, st.c0, st.u, st.kb, st.hh, st.ab, st.qb0
            par, p3 = u % 2, u % 3
            P = psp[par]
            for qb in range(qb0, 4):
                p.add("pe", lambda e, qb=qb: e.matmul(P[:, 256 + qb:256 + qb + 1], lhsT=spm3[p3][:, qb * 128:(qb + 1) * 128],
                                                      rhs=ac.negones[:, 0:1], start=True, stop=True),
                      reads=[("spm", p3), "negones"], writes=[("psp", par)])
            for qb in range(qb0, 4):
                p.add("pe", lambda e, qb=qb: e.matmul(P[:, qb * 64:(qb + 1) * 64], lhsT=wl[par][:, qb * 128:(qb + 1) * 128],
                                                      rhs=vv[b][:, kb, hh * 64:(hh + 1) * 64], start=True, stop=True),
                      reads=[("wl", par), ("vv", b)], writes=[("psp", par)])
            if kb > 0:
                p.add("act", lambda e: e.activation(out=gg[par][:, qb0:4], in_=P[:, 256 + qb0:260], func=AF.Exp),
                      reads=[("psp", par)], writes=[("gg", par)])
            A = acc[ab]
            for qb in range(qb0, 4):
                if kb == 0:
                    p.add("dve", lambda e, qb=qb: e.tensor_copy(out=A[:, qb, :], in_=P[:, qb * 64:(qb + 1) * 64]),
                          reads=[("psp", par)], writes=[("acc", ab, qb)])
                else:
                    p.add("dve", lambda e, qb=qb: e.scalar_tensor_tensor(
                        out=A[:, qb, :], in0=A[:, qb, :], scalar=gg[par][:, qb:qb + 1], in1=P[:, qb * 64:(qb + 1) * 64],
                        op0=ALU.mult, op1=ALU.add),
                        reads=[("psp", par), ("gg", par), ("acc", ab, qb)], writes=[("acc", ab, qb)])
            if st.last:
                hp, qc = st.hp, st.qc
                ob = (hp * ntiles + qc) % 2
                O = oc[ob]
                p.add("dve", lambda e: e.tensor_copy(out=O[:, :, hh * 64:(hh + 1) * 64], in_=acc[ab][:]),
                      reads=[("acc", ab, qb) for qb in range(4)], writes=[("oc", ob, hh)])
                if hh == 1:
                    p.add("aq", lambda e: e.dma_start(out=dap(g.o_s, qc * T * D + hp * 128, [[D, 128], [128 * D, 4], [1, 128]]), in_=O[:]),
                          reads=[("oc", ob, 0), ("oc", ob, 1)])

        steps = []
        nchunk = 0
        for hp in range(8):
            for qc in range(ntiles):
                for hh in range(2):
                    nk = 4 * qc + 4
                    for kb in range(nk):
                        steps.append(mk(hp, hh, qc, kb, nchunk % 2, len(steps), kb == nk - 1))
                    nchunk += 1
        N = len(steps)
        loaded = set()
        for n in range(N + 2):
            if n < N:
                st = steps[n]
                if st.hp not in loaded:
                    load_pair(st.hp)
                    loaded.add(st.hp)
                stageA(st)
            if 0 <= n - 1 < N:
                stageB1(steps[n - 1])
            if 0 <= n - 2 < N:
                stageB2(steps[n - 2])
        p.emit()


def attn_out_pass(C, g, cs, l, src, dst, ntiles=NT):
    nc = C.nc
    p = Prog(C)
    with contextlib.ExitStack() as es:
        sb = lambda n, s, dt: es.enter_context(nc.sbuf_tensor(C.name(n), list(s), dt))
        ps = lambda n: es.enter_context(nc.psum_tensor(C.name(n), [128, 512], F32))
        io = TileIO(C, p, es, g, cs, "hT", src, dst, l, 1)
        wo = sb("wo", [128, KC, D], BF16)
        ot = [sb("otk", [128, 4, D], BF16) for _ in range(2)]
        oT = [sb("oT", [128, KC, T], BF16) for _ in range(2)]
        pst = [es.enter_context(nc.psum_tensor(C.name("pstb"), [128, 1024], BF16)) for _ in range(2)]
        pso = [ps("pso") for _ in range(2)]
        for kc in range(KC):
            p.add("sp", lambda e, kc=kc: e.dma_start(out=wo[:, kc, :], in_=dap(g.sb_wo_bf, kc * 128 * D, [[D, 128], [1, D]])),
                  writes=[("wo", kc)])
        cnt = [0, 0]

        def tile(i):
            b = i % 2
            p.add("sp", lambda e: e.dma_start(out=ot[b][:], in_=dap(g.o_s, i * T * D, [[D, 128], [128 * D, 4], [1, D]])),
                  writes=[("ot", b)])
            io.load(i)
            for kc in range(KC):
                u = cnt[0]
                cnt[0] += 1
                par = u % 2
                for s4 in range(4):
                    p.add("pe", lambda e, kc=kc, s4=s4, par=par: e.transpose(
                        pst[par][:, s4 * 128:(s4 + 1) * 128], ot[b][:, s4, kc * 128:(kc + 1) * 128], cs.ident_bf[:]),
                        reads=[("ot", b), "ident_bf"], writes=[("pst", par)])
                if u % 2 == 0:
                    p.add("act", lambda e, kc=kc, par=par: e.activation(out=oT[b][:, kc, :], in_=pst[par][:, 0:T], func=AF.Copy),
                          reads=[("pst", par)], writes=[("oT", b, kc)])
                else:
                    p.add("dve", lambda e, kc=kc, par=par: e.tensor_copy(out=oT[b][:, kc, :], in_=pst[par][:, 0:T]),
                          reads=[("pst", par)], writes=[("oT", b, kc)])
            for dc in range(KC):
                po = cnt[1] % 2
                cnt[1] += 1
                for kc in range(KC):
                    p.add("pe", lambda e, kc=kc, dc=dc, po=po: e.matmul(
                        pso[po][:], lhsT=wo[:, kc, dc * 128:(dc + 1) * 128], rhs=oT[b][:, kc, :],
                        start=(kc == 0), stop=(kc == KC - 1)),
                        reads=[("wo", kc), ("oT", b, kc)], writes=[("pso", po)])
                io.residual(i, dc, pso[po], ("pso", po))
            io.store(i)
        for i in range(ntiles):
            tile(i)
        p.emit()


def hybrid_pass(C, g, cs, l, src, dst, ntiles=NT, extra_casts=()):
    nc = C.nc
    p = Prog(C)
    with contextlib.ExitStack() as es:
        sb = lambda n, s, dt: es.enter_context(nc.sbuf_tensor(C.name(n), list(s), dt))
        ps = lambda n, w=512, dt=F32: es.enter_context(nc.psum_tensor(C.name(n), [128, w], dt))
        io = TileIO(C, p, es, g, cs, "hT", src, dst, l, 1, nbuf=1)
        tri = sb("tri", [128, 128], F32)
        onesf = sb("onesf", [128, 128], F32)
        a_bc = sb("a_bc", [128, 16], F32)
        dtb_bc = sb("dtb_bc", [128, 16], F32)
        d_bc = sb("d_bc", [128, 16], F32)
        dmat = sb("dmat", [128, 16, 128], BF16)
        vngb = sb("vngb", [128, D], F32)
        ngb = sb("ngb", [128, D], F32)
        browf = sb("browf", [1, D], F32)
        brow = sb("brow", [1, D], BF16)
        onesrow = sb("onesrow", [1, 128], BF16)
        wmT = sb("wmT", [128, 8, 128], BF16)
        convw = sb("convw", [128, 48], F32)
        convb = sb("convb", [128, 12], F32)
        wbuf = [sb("wbuf", [128, KC, 512], BF16) for _ in range(2)]
        wob = sb("wob", [128, 16, 256], BF16)
        uT = sb("uT", [128, 8, T], BF16)
        vg = sb("vg", [128, 4, D], BF16)
        zs = sb("zs", [128, 4, D], BF16)
        xr = [sb("xr", [128, T + 3], F32) for _ in range(2)]
        tail = sb("tail", [128, 12, 3], F32)
        cacc = [sb("cacc", [128, T], F32) for _ in range(2)]
        xc = sb("xc", [128, 8, T], BF16)
        bT = sb("bT", [128, 2, T], BF16)
        cT = sb("cT", [128, 2, T], BF16)
        yab = sb("yab", [128, 16, T], BF16)
        dtt = sb("dtt", [128, 4, 16], F32)
        dA = sb("dA", [128, 4, 16], F32)
        junk = sb("junk", [128, 8, 128], BF16)
        ssv = sb("ssv", [128, 8], F32)
        vtmp = sb("vtmp", [128, D], F32)
        wsraw = vtmp
        vn = sb("vn", [128, 8, 128], BF16)
        xs = sb("xs", [128, D], BF16)
        xdt = sb("xdt", [128, D], BF16)
        xdd = sb("xdd", [128, D], BF16)
        btok = sb("btok", [128, 2, 128], BF16)
        rhsA = sb("rhsA", [128, 8, 128], F32)
        cst = sb("cst", [128, 16], F32)
        seg = [sb("seg", [128, 8, 128], F32) for _ in range(2)]
        eb = [sb("eb", [128, 8, 128], F32) for _ in range(2)]
        mT = sb("mT", [128, 16, 128], BF16)
        csm = sb("csm", [128, 16, 128], BF16)
        cbm = sb("cbm", [128, 2, 128], F32)
        yz = sb("yz", [128, D], F32)
        ssy = sb("ssy", [128, 2], F32)
        yb = sb("yb", [128, D], BF16)
        hin = sb("hin", [128, D], F32)
        hbf = sb("hbf", [128, D], BF16)
        psP = [ps("psP") for _ in range(2)]
        psC = ps("psC", 1024)
        psY = ps("psY", 1024)
        psM = ps("psM")
        cnt = [0, 0]

        regcache = {}

        def fillreg(e):
            if "r" not in regcache:
                regcache["r"] = e.to_reg(-30000.0)
            return regcache["r"]

        def P_():
            u = cnt[0]
            cnt[0] += 1
            return psP[u % 2], ("psP", u % 2)

        def Pb():
            pt, pk = P_()
            return pt.bitcast(BF16), pk

        p.add("pool", lambda e: e.memset(tri[:], 1.0), writes=["tri"])
        DEFER = list(extra_casts)
        p.add("pool", lambda e: e.affine_select(out=tri[:], in_=tri[:], pattern=[[1, 128]], compare_op=ALU.is_ge, fill=0.0,
                                                base=0, channel_multiplier=-1), reads=["tri"], writes=["tri"])
        p.add("pool", lambda e: e.memset(onesf[:], 1.0), writes=["onesf"])
        p.add("dve", lambda e: e.memset(onesrow[:], 1.0), writes=["onesrow"])
        p.add("dve", lambda e: e.memset(hin[:], 0.0), writes=["hin"])
        p.add("dve", lambda e: e.memset(hbf[:], 0.0), writes=["hbf"])
        p.add("dve", lambda e: e.memset(tail[:], 0.0), writes=["tail"])
        bc = lambda t, n: dap(t, 0, [[0, 128], [1, n]])
        p.add("sp", lambda e: e.dma_start(out=a_bc[:], in_=bc(g.a_log, 16)), writes=["a_bc"])
        p.add("sp", lambda e: e.dma_start(out=dtb_bc[:], in_=bc(g.dt_bias, 16)), writes=["dtb_bc"])
        p.add("sp", lambda e: e.dma_start(out=d_bc[:], in_=bc(g.ssd_d, 16)), writes=["d_bc"])
        p.add("sp", lambda e: e.dma_start(out=vngb[:], in_=bc(g.gm_vg, D)), writes=["vngb"])
        p.add("sp", lambda e: e.dma_start(out=ngb[:], in_=bc(g.ssd_ng, D)), writes=["ngb"])
        p.add("sp", lambda e: e.dma_start(out=browf[:], in_=g.gm_bs.ap()), writes=["browf"])
        p.add("sp", lambda e: e.dma_start(out=convw[:], in_=g.conv_wT.ap()), writes=["convw"])
        p.add("sp", lambda e: e.dma_start(out=convb[:], in_=g.conv_bT.ap()), writes=["convb"])
        p.add("sp", lambda e: e.dma_start(out=sap(wsraw, 0, [[128, 8], [1, 128]]), in_=dap(g.gm_ws, 0, [[128, 128], [128 * 128, 8], [1, 128]])), writes=["wsraw", "vtmp"])
        p.add("dve", lambda e: e.tensor_copy(out=brow[:], in_=browf[:]), reads=["browf"], writes=["brow"])
        p.add("act", lambda e: e.activation(out=a_bc[:], in_=a_bc[:], func=AF.Exp), reads=["a_bc"], writes=["a_bc"])
        p.add("dve", lambda e: e.tensor_scalar(out=a_bc[:], in0=a_bc[:], scalar1=-1.0, scalar2=None, op0=ALU.mult),
              reads=["a_bc"], writes=["a_bc"])
        p.add("dve", lambda e: e.tensor_tensor(out=dmat[:], in0=sap(cs.ident, 0, [[0, 16], [1, 128]]),
                                               in1=sap(d_bc, 0, [[1, 16], [0, 128]]), op=ALU.mult),
              reads=["d_bc"], writes=["dmat"])
        p.add("dve", lambda e: e.memset(sap(wsraw, 64, [[128, 8], [1, 64]], p0=0, pn=64), 0.0), reads=["wsraw"], writes=["wsraw"])
        for gi in range(8):
            pt, pk = P_()
            p.add("pe", lambda e, gi=gi, pt=pt: e.transpose(pt[:, 0:128], wsraw[:, gi * 128:(gi + 1) * 128], cs.ident[:]), reads=["wsraw", "vtmp"], writes=[pk])
            p.add("dve", lambda e, gi=gi, pt=pt: e.tensor_copy(out=wmT[:, gi, :], in_=pt[:, 0:128]), reads=[pk], writes=["wmT"])

        deferred = []
        emit_casts(p, g, DEFER, collect=deferred)
        per_blk = (len(deferred) + 4 * ntiles - 1) // (4 * ntiles)

        def load_w(c0, ncols, par):
            p.add("sp", lambda e: e.dma_start(out=wbuf[par][:, :, 0:ncols],
                                              in_=dap(g.hy_w_in_bf, c0, [[IN_W, 128], [128 * IN_W, KC], [1, ncols]])),
                  writes=[("wbuf", par)])

        def featmajor_group(yT, par, jj):
            pt, pk = P_()
            for kc in range(KC):
                p.add("pe", lambda e, kc=kc: e.matmul(pt[:], lhsT=wbuf[par][:, kc, jj * 128:(jj + 1) * 128], rhs=yT[:, kc, :],
                                                      start=(kc == 0), stop=(kc == KC - 1)),
                      reads=[("wbuf", par), ("yT", 0, kc)], writes=[pk])
            return pt, pk

        def tokmajor_group(yT, par, s4, ncols=512):
            pt, pk = P_()
            for kc in range(KC):
                p.add("pe", lambda e, kc=kc: e.matmul(pt[:, 0:ncols], lhsT=yT[:, kc, s4 * 128:(s4 + 1) * 128], rhs=wbuf[par][:, kc, 0:ncols],
                                                      start=(kc == 0), stop=(kc == KC - 1)),
                      reads=[("wbuf", par), ("yT", 0, kc)], writes=[pk])
            return pt, pk

        def conv_chunk(ch, pt, pk):
            par = ch % 2
            X, A = xr[par], cacc[par]
            xk, ak = ("xr", par), ("cacc", par)
            p.add("act", lambda e: e.activation(out=X[:, 3:T + 3], in_=pt[:], func=AF.Copy), reads=[pk], writes=[xk])
            te = "pool"
            p.add(te, lambda e: e.tensor_copy(out=X[:, 0:3], in_=tail[:, ch, :]), reads=[("tail", ch)], writes=[xk])
            p.add(te, lambda e: e.tensor_copy(out=tail[:, ch, :], in_=X[:, T:T + 3]), reads=[xk], writes=[("tail", ch)])
            p.add("dve", lambda e: e.tensor_scalar(out=A[:], in0=X[:, 3:T + 3], scalar1=convw[:, ch * 4 + 3:ch * 4 + 4], scalar2=None,
                                                   op0=ALU.mult), reads=[xk, "convw"], writes=[ak])
            for tap in range(3):
                p.add("dve", lambda e, tap=tap: e.scalar_tensor_tensor(
                    out=A[:], in0=X[:, tap:tap + T], scalar=convw[:, ch * 4 + tap:ch * 4 + tap + 1], in1=A[:],
                    op0=ALU.mult, op1=ALU.add), reads=[xk, ak, "convw"], writes=[ak])
            if ch < 8:
                dst_ap, dk = xc[:, ch, :], ("xc", ch)
            elif ch < 10:
                dst_ap, dk = bT[:, ch - 8, :], ("bT", ch - 8)
            else:
                dst_ap, dk = cT[:, ch - 10, :], ("cT", ch - 10)
            p.add("act", lambda e: e.activation(out=dst_ap, in_=A[:], func=AF.Silu, bias=convb[:, ch:ch + 1]),
                  reads=[ak, "convb"], writes=[dk])

        def gmlp_block(s4):
            blk = slice(s4 * 128, (s4 + 1) * 128)
            for gi in range(8):
                p.add("act", lambda e, gi=gi: e.activation(out=junk[:, gi, :], in_=vg[:, s4, gi * 128:(gi + 1) * 128], func=AF.Square,
                                                           accum_out=ssv[:, gi:gi + 1]),
                      reads=[("vg", s4)], writes=[("ssv", gi), ("junk", gi)])
            yield
            p.add("act", lambda e: e.activation(out=ssv[:], in_=ssv[:], func=AF.Ln, scale=1.0 / 128, bias=EPS), reads=[("ssv", gi) for gi in range(8)], writes=["ssv"])
            p.add("act", lambda e: e.activation(out=ssv[:], in_=ssv[:], func=AF.Exp, scale=-0.5), reads=["ssv"], writes=["ssv"] + [("ssv", gi) for gi in range(8)])
            p.add("dve", lambda e: e.tensor_tensor(out=vtmp[:], in0=vg[:, s4, :], in1=sap(ssv, 0, [[1, 8], [0, 128]]), op=ALU.mult),
                  reads=[("vg", s4), "ssv"], writes=["vtmp"])
            yield
            p.add("dve", lambda e: e.tensor_tensor(out=vn[:], in0=vtmp[:], in1=vngb[:], op=ALU.mult),
                  reads=["vtmp", "vngb"], writes=["vn"])
            yield
            for half in range(2):
                if half == 1:
                    yield
                pt, pk = P_()
                for q in range(4):
                    gi = half * 4 + q
                    p.add("pe", lambda e, gi=gi, q=q, pt=pt: e.matmul(pt[:, q * 128:(q + 1) * 128], lhsT=vn[:, gi, :], rhs=wmT[:, gi, :],
                                                               start=True, stop=False), reads=["vn", "wmT"], writes=[pk])
                    p.add("pe", lambda e, gi=gi, q=q, pt=pt: e.matmul(pt[:, q * 128:(q + 1) * 128], lhsT=onesrow[:], rhs=brow[:, gi * 128:(gi + 1) * 128],
                                                               start=False, stop=True), reads=["onesrow", "brow"], writes=[pk])
                p.add("dve", lambda e, half=half, pt=pt: e.tensor_tensor(
                    out=yab[:, half * 4:half * 4 + 4, blk], in0=uT[:, half * 4:half * 4 + 4, blk],
                    in1=sap(pt, 0, [[128, 4], [1, 128]]), op=ALU.mult),
                    reads=[pk] + [("uT", half * 4 + q) for q in range(4)], writes=[("yab", half * 4 + q) for q in range(4)])

        def ssd_chunk(s4, first):
            blk = slice(s4 * 128, (s4 + 1) * 128)
            psTb, tk = Pb()
            for kc in range(8):
                p.add("pe", lambda e, kc=kc: e.transpose(psTb[:, kc * 128:(kc + 1) * 128], xc[:, kc, blk], cs.ident_bf[:]),
                      reads=[("xc", kc)], writes=[tk])
            p.add("act", lambda e: e.activation(out=xs[:], in_=psTb[:], func=AF.Copy), reads=[tk], writes=["xs"])
            p.add("dve", lambda e: e.tensor_tensor(out=xdt[:], in0=psTb[:], in1=sap(dtt, s4 * 16, [[1, 16], [0, 64]]), op=ALU.mult),
                  reads=[tk, ("dtt", s4)], writes=["xdt"])
            yield
            psTb2, tk2 = Pb()
            for gi in range(2):
                p.add("pe", lambda e, gi=gi: e.transpose(psTb2[:, gi * 128:(gi + 1) * 128], bT[:, gi, blk], cs.ident_bf[:]),
                      reads=[("bT", gi)], writes=[tk2])
            p.add("act", lambda e: e.activation(out=btok[:], in_=psTb2[:, 0:256], func=AF.Copy), reads=[tk2], writes=["btok"])
            yield
            psQ = io.psn
            for gi in range(2):
                p.add("pe", lambda e, gi=gi: e.matmul(psQ[:, gi * 128:(gi + 1) * 128], lhsT=bT[:, gi, blk], rhs=cT[:, gi, blk],
                                                      start=True, stop=True), reads=[("bT", gi), ("cT", gi)], writes=["psn"])
            p.add("pe", lambda e: e.matmul(psQ[:, 256:272], lhsT=tri[:], rhs=dA[:, s4, :], start=True, stop=True),
                  reads=["tri", ("dA", s4)], writes=["psn"])
            p.add("dve", lambda e: e.tensor_tensor(out=cbm[:], in0=psQ[:, 0:256], in1=sap(tri, 0, [[0, 2], [1, 128]]), op=ALU.mult),
                  reads=["psn", "tri"], writes=["cbm"])
            p.add("act", lambda e: e.activation(out=cst[:], in_=psQ[:, 256:272], func=AF.Copy), reads=["psn"], writes=["cst"])
            banks = [(psC, 0, ("psC", 0)), (psC, 512, ("psC", 1)), (psM, 0, "psM")]
            for hg in range(4):
                yield
                gi = hg // 2
                h0 = hg * 4
                S_ = sap(seg[gi], (hg % 2) * 512, [[128, 4], [1, 128]])
                E_ = sap(eb[gi], (hg % 2) * 512, [[128, 4], [1, 128]])
                sk, ek = ("seg", hg), ("eb", hg)
                R_ = sap(rhsA, (hg % 2) * 512, [[128, 4], [1, 128]])
                rk = ("rhsA", hg % 2)
                bt, bo, bk = banks[cnt[1] % 3]
                cnt[1] += 1
                B_ = sap(bt, bo, [[128, 4], [1, 128]])
                p.add("dve", lambda e, h0=h0, R_=R_: e.tensor_tensor(out=R_, in0=sap(dA, s4 * 16 + h0, [[1, 4], [0, 128]]),
                                                                     in1=sap(tri, 0, [[0, 4], [1, 128]]), op=ALU.mult),
                      reads=[("dA", s4), "tri"], writes=[rk])
                p.add("pe", lambda e, hg=hg, bt=bt, bo=bo: e.matmul(bt[:, bo:bo + 512], lhsT=onesf[:],
                                                                    rhs=sap(rhsA, (hg % 2) * 512, [[1, 512]]), start=True, stop=True),
                      reads=[rk, "onesf"], writes=[bk])
                yield
                p.add("dve", lambda e, h0=h0, S_=S_, B_=B_: e.tensor_tensor(out=S_, in0=B_, in1=sap(cst, h0, [[1, 4], [0, 128]]), op=ALU.subtract),
                      reads=[bk, "cst"], writes=[sk])
                p.add("dve", lambda e, S_=S_: e.tensor_scalar(out=S_, in0=S_, scalar1=0.0, scalar2=None, op0=ALU.min),
                      reads=[sk], writes=[sk])
                p.add("act", lambda e, S_=S_: e.activation(out=S_, in_=S_, func=AF.Exp), reads=[sk], writes=[sk])
                p.add("act", lambda e, E_=E_, B_=B_: e.activation(out=E_, in_=B_, func=AF.Exp), reads=[bk], writes=[ek])
                yield
                p.add("dve", lambda e, gi=gi, h0=h0, S_=S_: e.tensor_tensor(out=mT[:, h0:h0 + 4, :], in0=S_,
                                                                            in1=sap(cbm, gi * 128, [[0, 4], [1, 128]]), op=ALU.mult),
                      reads=[sk, "cbm"], writes=[("mT", hg)])
                if not first:
                    p.add("dve", lambda e, gi=gi, h0=h0, E_=E_: e.tensor_tensor(out=csm[:, h0:h0 + 4, :], in0=E_,
                                                                                in1=sap(cT, gi * T + s4 * 128, [[0, 4], [1, 128]]), op=ALU.mult),
                          reads=[ek, ("cT", gi)], writes=[("csm", hg)])
            yield
            for h in range(16):
                gi = h // 8
                cols = slice(h * 64, (h + 1) * 64)
                p.add("pe", lambda e, h=h, cols=cols: e.matmul(psY[:, cols], lhsT=mT[:, h, :], rhs=xdt[:, cols], start=True, stop=False),
                      reads=[("mT", h // 4), "xdt"], writes=["psY"])
                p.add("pe", lambda e, h=h, cols=cols: e.matmul(psY[:, cols], lhsT=dmat[:, h, :], rhs=xs[:, cols], start=False, stop=first),
                      reads=["dmat", "xs"], writes=["psY"])
                if not first:
                    p.add("pe", lambda e, h=h, cols=cols: e.matmul(psY[:, cols], lhsT=csm[:, h, :], rhs=hbf[:, cols], start=False, stop=True),
                          reads=[("csm", h // 4), "hbf"], writes=["psY"])
            yield
            p.add("dve", lambda e: e.tensor_tensor(out=yz[:], in0=psY[:], in1=zs[:, s4, :], op=ALU.mult),
                  reads=["psY", ("zs", s4)], writes=["yz"])
            p.add("act", lambda e: e.activation(out=yb[:], in_=yz[:], func=AF.Square, accum_out=ssy[:, 0:1]),
                  reads=["yz"], writes=["ssy", "yb"])
            yield
            p.add("act", lambda e: e.activation(out=ssy[:, 1:2], in_=ssy[:, 0:1], func=AF.Ln, scale=1.0 / D, bias=EPS), reads=["ssy"], writes=["ssy"])
            p.add("act", lambda e: e.activation(out=ssy[:, 1:2], in_=ssy[:, 1:2], func=AF.Exp, scale=-0.5), reads=["ssy"], writes=["ssy"])
            p.add("dve", lambda e: e.scalar_tensor_tensor(out=yb[:], in0=yz[:], scalar=ssy[:, 1:2], in1=ngb[:], op0=ALU.mult, op1=ALU.mult),
                  reads=["yz", "ssy", "ngb"], writes=["yb"])
            yield
            psTb3, tk3 = Pb()
            for kc in range(8):
                p.add("pe", lambda e, kc=kc: e.transpose(psTb3[:, kc * 128:(kc + 1) * 128], yb[:, kc * 128:(kc + 1) * 128], cs.ident_bf[:]),
                      reads=["yb"], writes=[tk3])
            p.add("act", lambda e: e.activation(out=yab[:, 8:16, blk], in_=sap(psTb3, 0, [[128, 8], [1, 128]]), func=AF.Copy),
                  reads=[tk3], writes=[("yab", 8 + q) for q in range(8)])
            yield
            for gi in range(2):
                p.add("dve", lambda e, gi=gi: e.tensor_tensor(out=xdd[:, gi * 512:(gi + 1) * 512], in0=xdt[:, gi * 512:(gi + 1) * 512],
                                                              in1=sap(seg[gi], 127, [[128, 8], [0, 64]]), op=ALU.mult),
                      reads=["xdt", ("seg", 2 * gi), ("seg", 2 * gi + 1)], writes=[("xdd", gi)])
                p.add("pe", lambda e, gi=gi: e.matmul(psC[:, gi * 512:(gi + 1) * 512], lhsT=btok[:, gi, :], rhs=xdd[:, gi * 512:(gi + 1) * 512],
                                                      start=True, stop=True), reads=["btok", ("xdd", gi)], writes=[("psC", gi)])
                p.add("dve", lambda e, gi=gi: e.tensor_tensor(out=hin[:, gi * 512:(gi + 1) * 512], in0=hin[:, gi * 512:(gi + 1) * 512],
                                                              in1=sap(eb[gi], 127, [[128, 8], [0, 64]]), op=ALU.mult),
                      reads=["hin", ("eb", 2 * gi), ("eb", 2 * gi + 1)], writes=["hin"])
            p.add("dve", lambda e: e.tensor_tensor(out=hin[:], in0=hin[:], in1=psC[:], op=ALU.add),
                  reads=["hin", ("psC", 0), ("psC", 1)], writes=["hin"])
            p.add("act", lambda e: e.activation(out=hbf[:], in_=hin[:], func=AF.Copy), reads=["hin"], writes=["hbf"])

        def tile(i):
            io.load(i)
            io.norm(i)
            yT = io.yT[0]
            wi = [0]

            def nextw(c0, ncols):
                par = wi[0] % 2
                wi[0] += 1
                load_w(c0, ncols, par)
                return par
            for half in range(2):
                par = nextw(half * 512, 512)
                for jj in range(4):
                    pt, pk = featmajor_group(yT, par, jj)
                    gi = half * 4 + jj
                    p.add("act", lambda e, gi=gi, pt=pt: e.activation(out=uT[:, gi, :], in_=pt[:], func=AF.Gelu),
                          reads=[pk], writes=[("uT", gi)])
            for which, dstt, fn, nm in ((1, vg, AF.Gelu, "vg"), (2, zs, AF.Silu, "zs")):
                for half in range(2):
                    par = nextw(which * D + half * 512, 512)
                    for s4 in range(4):
                        pt, pk = tokmajor_group(yT, par, s4)
                        p.add("act", lambda e, s4=s4, half=half, pt=pt, dstt=dstt, fn=fn: e.activation(
                            out=dstt[:, s4, half * 512:(half + 1) * 512], in_=pt[:], func=fn), reads=[pk], writes=[(nm, s4)])
            par = nextw(4608, 16)
            for s4 in range(4):
                pt, pk = tokmajor_group(yT, par, s4, ncols=16)
                p.add("dve", lambda e, s4=s4, pt=pt: e.tensor_tensor(out=dtt[:, s4, :], in0=pt[:, 0:16], in1=dtb_bc[:], op=ALU.add),
                      reads=[pk, "dtb_bc"], writes=[("dtt", s4)])
            p.add("act", lambda e: e.activation(out=dtt[:], in_=dtt[:], func=AF.Exp), reads=[("dtt", s4) for s4 in range(4)],
                  writes=[("dtt", s4) for s4 in range(4)])
            p.add("act", lambda e: e.activation(out=dtt[:], in_=dtt[:], func=AF.Ln, bias=1.0), reads=[("dtt", s4) for s4 in range(4)],
                  writes=[("dtt", s4) for s4 in range(4)])
            p.add("dve", lambda e: e.tensor_tensor(out=dA[:], in0=dtt[:], in1=sap(a_bc, 0, [[0, 4], [1, 16]]), op=ALU.mult),
                  reads=[("dtt", s4) for s4 in range(4)] + ["a_bc"], writes=[("dA", s4) for s4 in range(4)])
            for grp in range(3):
                par = nextw(3 * D + grp * 512, 512)
                for jj in range(4):
                    ch = grp * 4 + jj
                    pt, pk = featmajor_group(yT, par, jj)
                    conv_chunk(ch, pt, pk)
            def run_gens(gens):
                gens = list(gens)
                while gens:
                    for gn in list(gens):
                        try:
                            next(gn)
                        except StopIteration:
                            gens.remove(gn)
            run_gens([gmlp_block(0)])
            for s4 in range(4):
                for _ in range(per_blk):
                    if deferred:
                        p.add("pq", deferred.pop(0), writes=[])
                gens = [ssd_chunk(s4, first=(i == 0 and s4 == 0))]
                if s4 + 1 < 4:
                    gens.append(gmlp_block(s4 + 1))
                run_gens(gens)
            for half in range(4):
                p.add("sp", lambda e, half=half: e.dma_start(out=wob[:], in_=dap(g.hy_w_out_bf, half * 256, [[D, 128], [128 * D, 16], [1, 256]])),
                      writes=["wob"])
                for dcl in range(2):
                    dc = half * 2 + dcl
                    pt, pk = P_()
                    for k16 in range(16):
                        p.add("pe", lambda e, k16=k16, dcl=dcl, pt=pt: e.matmul(
                            pt[:], lhsT=wob[:, k16, dcl * 128:(dcl + 1) * 128], rhs=yab[:, k16, :], start=(k16 == 0), stop=(k16 == 15)),
                            reads=["wob", ("yab", k16)], writes=[pk])
                    io.residual(i, dc, pt, pk)
            io.store(i)
            if hasattr(g, "dbgb") and i == 0:
                p.add("sp", lambda e: e.dma_start(out=g.dbgb.ap(), in_=yab[:]), reads=[("yab", k) for k in range(16)])
        for i in range(ntiles):
            tile(i)
        p.emit()


def out_pass(C, g, cs, src, l, ntiles=NT):
    nc = C.nc
    p = Prog(C)
    with contextlib.ExitStack() as es:
        sb = lambda n, s, dt: es.enter_context(nc.sbuf_tensor(C.name(n), list(s), dt))
        ps = lambda n: es.enter_context(nc.psum_tensor(C.name(n), [128, 512], F32))
        io = TileIO(C, p, es, g, cs, "hT", src, None, l, 0)
        ot = [sb("ot", [128, 4, D], F32) for _ in range(2)]
        pst = [ps("pst") for _ in range(2)]
        io.load(0)
        evc = [0]

        def tile(i):
            if i + 1 < ntiles:
                io.load(i + 1)
            b = i % io.nbuf
            hT = io.hT[b]
            io._rstd(b)
            for kc in range(KC):
                gcol = cs.normg[:, (l * 4 + 3) * KC + kc:(l * 4 + 3) * KC + kc + 1]
                p.add("dve", lambda e, kc=kc, gcol=gcol: e.scalar_tensor_tensor(
                    out=hT[:, kc, :], in0=hT[:, kc, :], scalar=gcol, in1=io.rstd[:], op0=ALU.mult, op1=ALU.mult),
                    reads=[("hT", b, kc), "rstd", "normg"], writes=[("hT", b, kc)])
            o = ot[i % 2]
            for s4 in range(4):
                for k2 in range(2):
                    ev = evc[0]
                    evc[0] += 1
                    pt = pst[ev % 2]
                    pk = ("pst", ev % 2)
                    for q in range(4):
                        kc = k2 * 4 + q
                        p.add("pe", lambda e, kc=kc, q=q, s4=s4, pt=pt: e.transpose(
                            pt[:, q * 128:(q + 1) * 128], hT[:, kc, s4 * 128:(s4 + 1) * 128], cs.ident[:]),
                            reads=[("hT", b, kc), "ident"], writes=[pk])
                    if ev % 2 == 0:
                        p.add("act", lambda e, pt=pt, s4=s4, k2=k2: e.activation(
                            out=o[:, s4, k2 * 512:(k2 + 1) * 512], in_=pt[:], func=AF.Copy),
                            reads=[pk], writes=[("ot", i % 2, s4)])
                    else:
                        p.add("dve", lambda e, pt=pt, s4=s4, k2=k2: e.tensor_copy(
                            out=o[:, s4, k2 * 512:(k2 + 1) * 512], in_=pt[:]),
                            reads=[pk], writes=[("ot", i % 2, s4)])
            p.add("aq", lambda e: e.dma_start(out=dap(g.out, i * T * D, [[D, 128], [128 * D, 4], [1, D]]), in_=o[:]),
                  reads=[("ot", i % 2, s4) for s4 in range(4)])
        for i in range(ntiles):
            tile(i)
        p.emit()


FFN_NEED = {"x", "cT", "mod_w0", "mod_bT", "norm_gT", "wg_00", "wu_00", "wd_00"}


def dbg_dump(C, g, cs):
    nc = C.nc
    p = Prog(C)
    p.add("sp", lambda e: e.dma_start(out=dap(g.dbg, 0, [[2048, 128], [1, 144]]), in_=cs.modT[:]))
    p.add("sp", lambda e: e.dma_start(out=dap(g.dbg, 144, [[2048, 128], [1, 48]]), in_=cs.A[:]))
    p.add("sp", lambda e: e.dma_start(out=dap(g.dbg, 192, [[2048, 128], [1, 48]]), in_=cs.gate[:]))
    p.add("sp", lambda e: e.dma_start(out=dap(g.dbg, 240, [[2048, 128], [1, 128]]), in_=cs.ident[:]))
    p.add("sp", lambda e: e.dma_start(out=dap(g.dbg, 368, [[2048, 128], [1, 8]]), in_=cs.condT[:]))
    p.emit()


def build(mode="full", ntiles=NT):
    nc = bass.Bass("TRN2", target_bir_lowering=False)
    need = None
    if mode in ("setup_test", "ffn_test"):
        need = FFN_NEED
    if mode == "hyb_test":
        need = {"hin", "cT", "mod_w0", "mod_bT", "norm_gT", "hy_w_in", "hy_w_out", "gm_v_norm_g", "gm_w_s", "gm_b_s", "conv_wT", "conv_bT",
                "ssd_dt_bias", "ssd_a_log", "ssd_d", "ssd_norm_g"}
    if mode == "attn_test":
        need = {"hin", "cT", "mod_w1", "mod_bT", "norm_gT", "sb_w_qkv", "sb_qgT", "sb_kgT", "sb_w_o"}
    g = declare_io(nc, mode, need)
    with contextlib.ExitStack() as top:
        C = Ctx(nc, top)
        cs = Consts(C, top)
        if mode in ("full", "full_test"):
            nt = ntiles
            setup_pass(C, g, cs, layers=(0, 1), cast=((0, 0),))
            ffn_pass(C, g, cs, 0, 0, "x", g.x, g.hT[0], ntiles=nt, extra_casts=("hy",))
            hybrid_pass(C, g, cs, 0, g.hT[0], g.hT[1], ntiles=nt, extra_casts=((0, 1), (1, 0), "sb", (1, 1)))
            ffn_pass(C, g, cs, 0, 1, "hT", g.hT[1], g.hT[0], ntiles=nt)
            ffn_pass(C, g, cs, 1, 0, "hT", g.hT[0], g.hT[1], prenorm=0, ntiles=nt)
            attn_qkv_pass(C, g, cs, 1, g.hT[1], ntiles=nt)
            attn_core_pass(C, g, cs, ntiles=nt)
            attn_out_pass(C, g, cs, 1, g.hT[1], g.hT[0], ntiles=nt)
            ffn_pass(C, g, cs, 1, 1, "hT", g.hT[0], g.hT[1], ntiles=nt)
            out_pass(C, g, cs, g.hT[1], 1, ntiles=nt)
        if mode == "setup_test":
            setup_pass(C, g, cs, layers=(0,), cast=("ffn",))
            dbg_dump(C, g, cs)
        if mode == "ffn_test":
            setup_pass(C, g, cs, layers=(0,), cast=("ffn",))
            ffn_pass(C, g, cs, 0, 0, "x", g.x, g.hT[0], ntiles=ntiles)
            out_pass(C, g, cs, g.hT[0], 0, ntiles=ntiles)
        if mode == "attn_test":
            setup_pass(C, g, cs, layers=(1,), cast=("sb",))
            attn_qkv_pass(C, g, cs, 1, g.hin, ntiles=ntiles)
            attn_core_pass(C, g, cs, ntiles=ntiles)
            attn_out_pass(C, g, cs, 1, g.hin, g.hT[0], ntiles=ntiles)
        if mode == "hyb_test":
            setup_pass(C, g, cs, layers=(0,), cast=("hy",))
            hybrid_pass(C, g, cs, 0, g.hin, g.hT[0], ntiles=ntiles)
        final_wait(C)
    return nc, g


def make_in_maps(inp):
    f = lambda a: np.ascontiguousarray(np.asarray(a, dtype=np.float32))
    shared = {
        "mod_w0": f(inp["mod_w"][0]), "mod_w1": f(inp["mod_w"][1]),
        "mod_bT": f(np.asarray(inp["mod_b"]).reshape(2, 72, 128).transpose(2, 0, 1).reshape(128, 144)),
        "norm_gT": f(np.asarray(inp["norm_g"]).reshape(2, 4, KC, 128).transpose(3, 0, 1, 2).reshape(128, 64)),
        "hy_w_in": f(inp["hy_w_in"][0]), "hy_w_out": f(inp["hy_w_out"][0]),
        "gm_v_norm_g": f(inp["gm_v_norm_g"]), "gm_w_s": f(inp["gm_w_s"][0]),
        "gm_b_s": f(np.asarray(inp["gm_b_s"]).reshape(1, 1024)),
        "conv_wT": f(np.asarray(inp["ssd_conv_w"][0]).reshape(4, 12, 128).transpose(2, 1, 0).reshape(128, 48)),
        "conv_bT": f(np.asarray(inp["ssd_conv_b"][0]).reshape(12, 128).T),
        "ssd_dt_bias": f(inp["ssd_dt_bias"]), "ssd_a_log": f(inp["ssd_a_log"]), "ssd_d": f(inp["ssd_d"]),
        "ssd_norm_g": f(inp["ssd_norm_g"]),
        "sb_w_qkv": f(inp["sb_w_qkv"][0]),
        "sb_qgT": f(np.asarray(inp["sb_q_norm_g"]).reshape(64, 1)), "sb_kgT": f(np.asarray(inp["sb_k_norm_g"]).reshape(64, 1)),
        "sb_w_o": f(inp["sb_w_o"][0]),
    }
    for l in range(2):
        for ff in range(2):
            shared["wg_%d%d" % (l, ff)] = f(inp["ffn_w_gate"][l, ff])
            shared["wu_%d%d" % (l, ff)] = f(inp["ffn_w_up"][l, ff])
            shared["wd_%d%d" % (l, ff)] = f(inp["ffn_w_down"][l, ff])
    maps = []
    x = np.asarray(inp["x"], dtype=np.float32)
    c = np.asarray(inp["c"], dtype=np.float32)
    for b in range(NCORES):
        m = dict(shared)
        m["x"] = np.ascontiguousarray(x[b])
        m["cT"] = np.ascontiguousarray(c[b].reshape(KC, 128).T)
        maps.append(m)
    return maps


def kernel(**inputs):
    nc, g = build("full")
    in_maps = make_in_maps(inputs)
    res = run_bass_kernel_spmd(nc, in_maps, core_ids=list(range(NCORES)))
    return np.stack([np.asarray(r["out"], dtype=np.float32) for r in res.results], axis=0)
```

```python
import contextlib
import numpy as np
import concourse.bass as bass
import concourse.mybir as mybir
from concourse.bass_utils import run_bass_kernel_spmd

F32 = mybir.dt.float32
BF16 = mybir.dt.bfloat16
AF = mybir.ActivationFunctionType
ALU = mybir.AluOpType

D = 1024
S = 4096
KC = 8
T = 512
NT = S // T
DFF = 2816
NJ = DFF // 128
EPS = 1e-6
IN_W = 4624
NCORES = 8
SKIP = set()

COMPUTE = ("pe", "act", "dve", "pool")
STREAM_OF = {"pe": "pe", "act": "act", "dve": "dve", "pool": "pool", "sp": "sp", "pq": "pool", "aq": "act"}
DMAQ = ("sp", "pq", "aq")
NS_DMA = 8


class Op:
    __slots__ = ("eng", "fn", "deps", "signal", "sigval", "isdma", "dmak")

    def __init__(self, eng, fn, isdma):
        self.eng = eng
        self.fn = fn
        self.deps = []
        self.signal = False
        self.sigval = 0
        self.isdma = isdma
        self.dmak = -1


class Ctx:
    def __init__(self, nc, stack):
        self.nc = nc
        self.sems = {}
        for e in COMPUTE:
            self.sems[e] = stack.enter_context(nc.semaphore("s_" + e))
        for q in DMAQ:
            for i in range(NS_DMA):
                self.sems[(q, i)] = stack.enter_context(nc.semaphore("d_%s%d" % (q, i)))
        self.sigcount = {e: 0 for e in COMPUTE}
        self.dma_count = {q: 0 for q in DMAQ}
        self.dma_ops = {q: [] for q in DMAQ}
        self.uid = 0
        self.bar_sb = stack.enter_context(nc.sbuf_tensor("bar_sb", [128, 8], F32))
        self.bar_bf = stack.enter_context(nc.sbuf_tensor("bar_bf", [128, 8], BF16))

    def name(self, s):
        self.uid += 1
        return "%s_%d" % (s, self.uid)


class Prog:
    def __init__(self, C):
        self.C = C
        self.nc = C.nc
        self.streams = {e: [] for e in ("pe", "act", "dve", "pool", "sp")}
        self.res = {}
        self.all_ops = []

    def add(self, eng, fn, reads=(), writes=()):
        C = self.C
        isdma = eng in DMAQ
        op = Op(eng, fn, isdma)
        deps = []
        raw = []
        pr = [k for k in reads if (k if isinstance(k, str) else k[0]).startswith("ps")]
        if pr:
            writes = list(writes) + [k for k in pr if k not in writes]
        for k in reads:
            r = self.res.get(k)
            if r is not None:
                for lst in r[0].values():
                    raw.extend(lst)
        for k in writes:
            r = self.res.get(k)
            if r is not None:
                for lst in r[0].values():
                    deps.extend(lst)
                for lst in r[1].values():
                    deps.extend(lst)
        seen = set()
        for d in raw:
            if id(d) in seen:
                continue
            seen.add(id(d))
            if (not d.isdma) and (not isdma) and d.eng == eng and eng == "pe":
                continue
            op.deps.append(d)
        for d in deps:
            if id(d) in seen:
                continue
            seen.add(id(d))
            if (not d.isdma) and (not isdma) and d.eng == eng and eng == "pe":
                continue
            op.deps.append(d)
        if isdma:
            k = C.dma_count[eng]
            C.dma_count[eng] = k + 1
            op.dmak = k
            if k >= NS_DMA:
                op.deps.append(C.dma_ops[eng][k - NS_DMA])
            C.dma_ops[eng].append(op)
        for k in reads:
            r = self.res.setdefault(k, [{}, {}])
            if isdma:
                r[1].setdefault(eng, []).append(op)
            else:
                r[1][eng] = [op]
        for k in writes:
            self.res[k] = [{eng: [op]}, {}]
        self.streams[STREAM_OF[eng]].append(op)
        self.all_ops.append(op)
        return op

    def emit(self):
        C = self.C
        nc = self.nc
        sems = C.sems
        bar = C.bar_sb
        barb = C.bar_bf
        endops = {}
        endops["act"] = self.add("act", lambda e: e.memzero(bar[:, 0:1]), writes=["__bar_act"])
        endops["dve"] = self.add("dve", lambda e: e.memset(bar[:, 2:3], 0.0), writes=["__bar_dve"])
        endops["pool"] = self.add("pool", lambda e: e.memset(bar[:, 3:4], 0.0), writes=["__bar_pool"])
        for o in endops.values():
            o.signal = True
        for ops in self.streams.values():
            for op in ops:
                for d in op.deps:
                    d.signal = True
        pe_ops = self.streams["pe"]
        if pe_ops:
            pe_ops[-1].signal = True
        for e in COMPUTE:
            cnt = C.sigcount[e]
            for op in self.streams[e]:
                if op.isdma:
                    continue
                if op.signal:
                    cnt += 1
                    op.sigval = cnt
            C.sigcount[e] = cnt
        start_wait = dict(getattr(C, "bar_vals", {}))
        start_dma = dict(getattr(C, "bar_dma", {}))

        def tok(d):
            if d.isdma:
                return sems[(d.eng, d.dmak % NS_DMA)], 16 * (d.dmak // NS_DMA + 1)
            return sems[d.eng], d.sigval

        def run(stream, engobj):
            waited = {}
            for e, v in start_wait.items():
                if e != stream and v > 0:
                    engobj.wait_ge(sems[e], v)
                    waited[sems[e].name] = v
                elif e == stream:
                    waited[sems[e].name] = v
            for (q, i), v in start_dma.items():
                if v > 0:
                    engobj.wait_ge(sems[(q, i)], v)
                    waited[sems[(q, i)].name] = v
            for op in self.streams[stream]:
                for d in op.deps:
                    sem, val = tok(d)
                    if waited.get(sem.name, 0) >= val:
                        continue
                    waited[sem.name] = val
                    engobj.wait_ge(sem, val)
                ins = op.fn(engobj)
                if op.isdma:
                    ins.then_inc(sems[(op.eng, op.dmak % NS_DMA)], 16)
                elif op.signal:
                    ins.then_inc(sems[op.eng], 1)

        with nc.Block() as block:
            @block.sync
            def _(e):
                run("sp", e)

            @block.tensor
            def _(e):
                run("pe", e)

            @block.scalar
            def _(e):
                run("act", e)

            @block.vector
            def _(e):
                run("dve", e)

            @block.gpsimd
            def _(e):
                run("pool", e)
        C.bar_vals = {e: C.sigcount[e] for e in COMPUTE}
        bd = {}
        for q in DMAQ:
            n = C.dma_count[q]
            for i in range(NS_DMA):
                cnt = (n - i + NS_DMA - 1) // NS_DMA if n > i else 0
                bd[(q, i)] = 16 * cnt
        C.bar_dma = bd


def final_wait(C):
    nc = C.nc
    with nc.Block() as block:
        @block.sync
        def _(e):
            for (q, i), v in C.bar_dma.items():
                if v > 0:
                    e.wait_ge(C.sems[(q, i)], v)
            for en, v in C.bar_vals.items():
                if v > 0:
                    e.wait_ge(C.sems[en], v)


def sap(t, off, dims, p0=0, pn=128):
    row = 1
    for s in t.shape[1:]:
        row *= s
    return bass.AP(t, p0 * row + off, [[row, pn]] + [list(d) for d in dims])


class K:
    pass


def declare_io(nc, mode, need=None):
    g = K()
    g.names = []

    def di(n, s, dt=F32):
        if need is not None and n not in need:
            return None
        g.names.append(n)
        return nc.dram_tensor(n, list(s), dt, kind="ExternalInput")
    g.x = di("x", [S, D])
    g.hin = di("hin", [D, S]) if (need is not None and "hin" in need) else None
    g.cT = di("cT", [128, KC])
    g.mod_w = [di("mod_w%d" % l, [D, 9 * D]) for l in range(2)]
    g.mod_bT = di("mod_bT", [128, 2 * 72])
    g.norm_gT = di("norm_gT", [128, 2 * 4 * KC])
    g.wg = {(l, f): di("wg_%d%d" % (l, f), [D, DFF]) for l in range(2) for f in range(2)}
    g.wu = {(l, f): di("wu_%d%d" % (l, f), [D, DFF]) for l in range(2) for f in range(2)}
    g.wd = {(l, f): di("wd_%d%d" % (l, f), [DFF, D]) for l in range(2) for f in range(2)}
    g.hy_w_in = di("hy_w_in", [D, IN_W])
    g.hy_w_out = di("hy_w_out", [2 * D, D])
    g.gm_vg = di("gm_v_norm_g", [1, D])
    g.gm_ws = di("gm_w_s", [8, 128, 128])
    g.gm_bs = di("gm_b_s", [1, 8 * 128])
    g.conv_wT = di("conv_wT", [128, 12 * 4])
    g.conv_bT = di("conv_bT", [128, 12])
    g.dt_bias = di("ssd_dt_bias", [1, 16])
    g.a_log = di("ssd_a_log", [1, 16])
    g.ssd_d = di("ssd_d", [1, 16])
    g.ssd_ng = di("ssd_norm_g", [1, D])
    g.sb_wqkv = di("sb_w_qkv", [D, 3 * D])
    g.sb_qg = di("sb_qgT", [64, 1])
    g.sb_kg = di("sb_kgT", [64, 1])
    g.sb_wo = di("sb_w_o", [D, D])
    g.out = nc.dram_tensor("out", [S, D], F32, kind="ExternalOutput")
    ds = lambda n, s, dt: nc.dram_tensor(n, list(s), dt, kind="Internal")
    if mode in ("full", "full_test"):
        g.hT = [ds("hT0", [D, S], F32), ds("hT1", [D, S], F32)]
    else:
        g.hT = [nc.dram_tensor("hT0", [D, S], F32, kind="ExternalOutput"), ds("hT1", [D, S], F32)]
        g.dbg = nc.dram_tensor("dbg", [128, 2048], F32, kind="ExternalOutput")
        g.dbgb = nc.dram_tensor("dbgb", [128, 16 * T], BF16, kind="ExternalOutput")
    has = lambda l, f: g.wg[(l, f)] is not None
    g.wg_bf = {(l, f): ds("wg_bf%d%d" % (l, f), [D, DFF], BF16) for l in range(2) for f in range(2) if has(l, f)}
    g.wu_bf = {(l, f): ds("wu_bf%d%d" % (l, f), [D, DFF], BF16) for l in range(2) for f in range(2) if has(l, f)}
    g.wd_bf = {(l, f): ds("wd_bf%d%d" % (l, f), [DFF, D], BF16) for l in range(2) for f in range(2) if has(l, f)}
    g.hy_w_in_bf = ds("hy_w_in_bf", [D, IN_W], BF16)
    g.hy_w_out_bf = ds("hy_w_out_bf", [2 * D, D], BF16)
    g.sb_wqkv_bf = ds("sb_wqkv_bf", [D, 3 * D], BF16)
    g.sb_wo_bf = ds("sb_wo_bf", [D, D], BF16)
    g.qT = ds("qT_s", [16, 64, S], BF16)
    g.kT = ds("kT_s", [16, 64, S], BF16)
    g.v_s = ds("v_s", [S, D], BF16)
    g.o_s = ds("o_s", [S, D], BF16)
    return g


def dap(t, off, dims):
    return bass.AP(t, off, [list(d) for d in dims])


def emit_casts(p, g, items, collect=None):
    def cast2d(src, dst, rows, cols):
        for r0 in range(0, rows, 128):
            fn = (lambda e, r0=r0: e.dma_start(
                out=dap(dst, r0 * cols, [[cols, 128], [1, cols]]),
                in_=dap(src, r0 * cols, [[cols, 128], [1, cols]])))
            if collect is None:
                p.add("pq", fn, writes=[])
            else:
                collect.append(fn)
    for it in items:
        if it == "ffn":
            for (l, f) in sorted(g.wg_bf.keys()):
                cast2d(g.wg[(l, f)], g.wg_bf[(l, f)], D, DFF)
                cast2d(g.wu[(l, f)], g.wu_bf[(l, f)], D, DFF)
                cast2d(g.wd[(l, f)], g.wd_bf[(l, f)], DFF, D)
        elif isinstance(it, tuple):
            l, f = it
            cast2d(g.wg[(l, f)], g.wg_bf[(l, f)], D, DFF)
            cast2d(g.wu[(l, f)], g.wu_bf[(l, f)], D, DFF)
            cast2d(g.wd[(l, f)], g.wd_bf[(l, f)], DFF, D)
        elif it == "hy":
            cast2d(g.hy_w_in, g.hy_w_in_bf, D, IN_W)
            cast2d(g.hy_w_out, g.hy_w_out_bf, 2 * D, D)
        elif it == "sb":
            cast2d(g.sb_wqkv, g.sb_wqkv_bf, D, 3 * D)
            cast2d(g.sb_wo, g.sb_wo_bf, D, D)


def setup_pass(C, g, cs, layers=(0, 1), cast=("ffn", "hy", "sb")):
    nc = C.nc
    p = Prog(C)
    with contextlib.ExitStack() as es:
        sb = lambda n, s, dt: es.enter_context(nc.sbuf_tensor(C.name(n), list(s), dt))
        emit_casts(p, g, cast)

        ident = cs.ident
        p.add("pool", lambda e: e.memset(ident[:], 1.0), writes=["ident"])
        p.add("pool", lambda e: e.affine_select(out=ident[:], in_=ident[:], pattern=[[-1, 128]], compare_op=ALU.is_equal,
                                                fill=0.0, base=0, channel_multiplier=1), reads=["ident"], writes=["ident"])
        p.add("dve", lambda e: e.tensor_copy(out=cs.ident_bf[:], in_=ident[:]), reads=["ident"], writes=["ident_bf"])
        p.add("dve", lambda e: e.memset(cs.ones_bf[:], 1.0), writes=["ones_bf"])
        p.add("sp", lambda e: e.dma_start(out=cs.condT[:], in_=g.cT.ap()), writes=["condT"])
        p.add("sp", lambda e: e.dma_start(out=cs.modT[:], in_=g.mod_bT.ap()), writes=["mod_b"])
        p.add("sp", lambda e: e.dma_start(out=cs.normg[:], in_=g.norm_gT.ap()), writes=["normg"])
        p.add("act", lambda e: e.activation(out=cs.condT[:], in_=cs.condT[:], func=AF.Silu), reads=["condT"], writes=["condT"])
        wbuf = [sb("modw", [128, KC, 512], F32) for _ in range(2)]
        psm = es.enter_context(nc.psum_tensor(C.name("psm"), [128, 512], F32))
        modb = sb("modb", [128, 144], F32)
        p.add("dve", lambda e: e.tensor_copy(out=modb[:], in_=cs.modT[:]), reads=["mod_b"], writes=["modb"])
        it = 0
        for l in layers:
            for cg in range(18):
                wb = wbuf[it % 2]
                key = "modw%d" % (it % 2)
                it += 1
                p.add("sp", lambda e, l=l, cg=cg, wb=wb: e.dma_start(
                    out=wb[:], in_=dap(g.mod_w[l], cg * 512, [[9 * D, 128], [128 * 9 * D, KC], [1, 512]])),
                    writes=[key])
                for c4 in range(4):
                    oc = cg * 4 + c4
                    col = l * 72 + oc
                    for kc in range(KC):
                        p.add("pe", lambda e, wb=wb, c4=c4, kc=kc, col=col: e.matmul(
                            psm[:, col:col + 1], lhsT=wb[:, kc, c4 * 128:(c4 + 1) * 128], rhs=cs.condT[:, kc:kc + 1],
                            start=(kc == 0), stop=(kc == KC - 1)), reads=[key, "condT"], writes=["psm"])
        for l in layers:
            p.add("dve", lambda e, l=l: e.tensor_tensor(out=cs.modT[:, l * 72:(l + 1) * 72], in0=psm[:, l * 72:(l + 1) * 72],
                                                        in1=modb[:, l * 72:(l + 1) * 72], op=ALU.add),
                  reads=["psm", "modb"], writes=["modT"])
        for l in layers:
            for sub in range(3):
                sc = cs.modT[:, l * 72 + sub * 24 + 8: l * 72 + sub * 24 + 16]
                ng = cs.normg[:, (l * 4 + sub) * KC:(l * 4 + sub + 1) * KC]
                a_out = cs.A[:, (l * 3 + sub) * KC:(l * 3 + sub + 1) * KC]
                p.add("dve", lambda e, sc=sc, ng=ng, a_out=a_out: e.scalar_tensor_tensor(
                    out=a_out, in0=sc, scalar=1.0, in1=ng, op0=ALU.add, op1=ALU.mult),
                    reads=["modT", "normg"], writes=["A"])
                gt = cs.modT[:, l * 72 + sub * 24 + 16: l * 72 + sub * 24 + 24]
                g_out = cs.gate[:, (l * 3 + sub) * KC:(l * 3 + sub + 1) * KC]
                p.add("dve", lambda e, gt=gt, g_out=g_out, sub=sub: e.tensor_scalar(
                    out=g_out, in0=gt, scalar1=(1.0 if sub == 1 else 0.5), scalar2=None, op0=ALU.mult),
                    reads=["modT"], writes=["gate"])
        p.emit()


class Consts:
    def __init__(self, C, stack):
        nc = C.nc
        sb = lambda n, s, dt: stack.enter_context(nc.sbuf_tensor(n, list(s), dt))
        self.ident = sb("ident", [128, 128], F32)
        self.ident_bf = sb("ident_bf", [128, 128], BF16)
        self.ones_bf = sb("ones_bf", [128, 128], BF16)
        self.condT = sb("condT", [128, KC], F32)
        self.modT = sb("modT", [128, 144], F32)
        self.normg = sb("normg", [128, 64], F32)
        self.A = sb("Amod", [128, 6 * KC], F32)
        self.gate = sb("gate", [128, 6 * KC], F32)

    def shift(self, l, sub):
        return self.modT[:, l * 72 + sub * 24: l * 72 + sub * 24 + 8]


class TileIO:
    def __init__(self, C, p, es, g, cs, src_kind, src, dst, l, sub, prenorm=None, nbuf=2):
        nc = C.nc
        self.C, self.p, self.g, self.cs = C, p, g, cs
        self.src_kind, self.src, self.dst = src_kind, src, dst
        self.l, self.sub, self.prenorm = l, sub, prenorm
        sb = lambda n, s, dt: es.enter_context(nc.sbuf_tensor(C.name(n), list(s), dt))
        ps = lambda n: es.enter_context(nc.psum_tensor(C.name(n), [128, 512], F32))
        self.nbuf = nbuf
        self.hT = [sb("hTt", [128, KC, T], F32) for _ in range(nbuf)]
        self.yT = [sb("yT", [128, KC, T], BF16) for _ in range(nbuf)]
        self.sq = sb("sq", [128, KC, T], BF16)
        self.rstd = sb("rstd", [128, T], F32)
        self.tmp = [sb("ntmp", [128, T], F32) for _ in range(2)]
        self.psn = ps("psn")
        if src_kind == "x":
            self.xin = sb("xin", [128, 4, D], F32)
            self.psT = ps("psT")

    def load(self, i):
        p, g = self.p, self.g
        b = i % self.nbuf
        hT = self.hT[b]
        if self.src_kind == "hT":
            p.add("sp", lambda e: e.dma_start(out=hT[:], in_=dap(self.src, i * T, [[S, 128], [128 * S, KC], [1, T]])),
                  writes=[("hT", b, kc) for kc in range(KC)])
        else:
            xin = self.xin
            p.add("sp", lambda e: e.dma_start(out=xin[:], in_=dap(self.src, i * T * D, [[D, 128], [128 * D, 4], [1, D]])),
                  writes=["xin"])
            for kc in range(KC):
                for s4 in range(4):
                    p.add("pe", lambda e, kc=kc, s4=s4: e.transpose(self.psT[:, s4 * 128:(s4 + 1) * 128],
                                                                      xin[:, s4, kc * 128:(kc + 1) * 128], self.cs.ident[:]),
                          reads=["xin", "ident"], writes=["psT"])
                if kc % 2 == 0:
                    p.add("act", lambda e, kc=kc: e.activation(out=hT[:, kc, :], in_=self.psT[:], func=AF.Copy),
                          reads=["psT"], writes=[("hT", b, kc)])
                else:
                    p.add("dve", lambda e, kc=kc: e.tensor_copy(out=hT[:, kc, :], in_=self.psT[:]),
                          reads=["psT"], writes=[("hT", b, kc)])

    def _rstd(self, b):
        p = self.p
        hT = self.hT[b]
        p.add("act", lambda e: e.activation(out=self.sq[:], in_=hT[:], func=AF.Square),
              reads=[("hT", b, kc) for kc in range(KC)], writes=["sq"])
        for kc in range(KC):
            p.add("pe", lambda e, kc=kc: e.matmul(self.psn[:], lhsT=self.cs.ones_bf[:], rhs=self.sq[:, kc, :],
                                                  start=(kc == 0), stop=(kc == KC - 1)),
                  reads=["sq", "ones_bf"], writes=["psn"])
        p.add("act", lambda e: e.activation(out=self.rstd[:], in_=self.psn[:], func=AF.Ln, scale=1.0 / D, bias=EPS),
              reads=["psn"], writes=["rstd"])
        p.add("act", lambda e: e.activation(out=self.rstd[:], in_=self.rstd[:], func=AF.Exp, scale=-0.5),
              reads=["rstd"], writes=["rstd"])

    def norm(self, i):
        p, cs = self.p, self.cs
        b = i % self.nbuf
        hT, yT = self.hT[b], self.yT[b]
        if self.prenorm is not None:
            self._rstd(b)
            pl = self.prenorm
            for kc in range(KC):
                gcol = cs.normg[:, (pl * 4 + 3) * KC + kc:(pl * 4 + 3) * KC + kc + 1]
                p.add("dve", lambda e, kc=kc, gcol=gcol: e.scalar_tensor_tensor(
                    out=hT[:, kc, :], in0=hT[:, kc, :], scalar=gcol, in1=self.rstd[:], op0=ALU.mult, op1=ALU.mult),
                    reads=[("hT", b, kc), "rstd", "normg"], writes=[("hT", b, kc)])
        self._rstd(b)
        l, sub = self.l, self.sub
        for kc in range(KC):
            acol = cs.A[:, (l * 3 + sub) * KC + kc:(l * 3 + sub) * KC + kc + 1]
            scol = cs.modT[:, l * 72 + sub * 24 + kc: l * 72 + sub * 24 + kc + 1]
            tmp = self.tmp[kc % 2]
            tk = "ntmp%d" % (kc % 2)
            p.add("dve", lambda e, kc=kc, acol=acol, tmp=tmp: e.scalar_tensor_tensor(
                out=tmp[:], in0=hT[:, kc, :], scalar=acol, in1=self.rstd[:], op0=ALU.mult, op1=ALU.mult),
                reads=[("hT", b, kc), "rstd", "A"], writes=[tk])
            p.add("act", lambda e, kc=kc, scol=scol, tmp=tmp: e.activation(
                out=yT[:, kc, :], in_=tmp[:], func=AF.Identity, bias=scol),
                reads=[tk, "modT"], writes=[("yT", b, kc)])

    def residual(self, i, dc, pso, pso_key):
        p, cs = self.p, self.cs
        b = i % self.nbuf
        hT = self.hT[b]
        l, sub = self.l, self.sub
        gcol = cs.gate[:, (l * 3 + sub) * KC + dc:(l * 3 + sub) * KC + dc + 1]
        p.add("dve", lambda e: e.scalar_tensor_tensor(out=hT[:, dc, :], in0=pso[:], scalar=gcol, in1=hT[:, dc, :],
                                                      op0=ALU.mult, op1=ALU.add),
              reads=[pso_key, ("hT", b, dc), "gate"], writes=[("hT", b, dc)])

    def store(self, i):
        p = self.p
        b = i % self.nbuf
        hT = self.hT[b]
        p.add("aq", lambda e: e.dma_start(out=dap(self.dst, i * T, [[S, 128], [128 * S, KC], [1, T]]), in_=hT[:]),
              reads=[("hT", b, kc) for kc in range(KC)])


def ffn_pass(C, g, cs, l, f, src_kind, src, dst, prenorm=None, ntiles=NT, extra_casts=()):
    nc = C.nc
    sub = 0 if f == 0 else 2
    NH = 2 if ntiles % 2 == 0 else 1
    p = Prog(C)
    with contextlib.ExitStack() as es:
        sb = lambda n, s, dt: es.enter_context(nc.sbuf_tensor(C.name(n), list(s), dt))
        ps = lambda n: es.enter_context(nc.psum_tensor(C.name(n), [128, 512], F32))
        io = TileIO(C, p, es, g, cs, src_kind, src, dst, l, sub, prenorm, nbuf=NH)
        a = sb("a", [128, NJ, NH, T], BF16)
        sg = [sb("sg", [128, T], F32) for _ in range(2)]
        wgb = [sb("wgb", [128, KC, 512], BF16) for _ in range(2)]
        wub = [sb("wub", [128, KC, 512], BF16) for _ in range(2)]
        wdb = [sb("wdb", [128, NJ, 512], BF16) for _ in range(2)]
        psg = [ps("psg") for _ in range(2)]
        psu = [ps("psu") for _ in range(2)]
        pso = [ps("pso") for _ in range(2)]
        wg_bf, wu_bf, wd_bf = g.wg_bf[(l, f)], g.wu_bf[(l, f)], g.wd_bf[(l, f)]
        groups = [(0, 4), (4, 4), (8, 4), (12, 4), (16, 4), (20, 2)]
        tasks = []
        cnt = {"g": 0, "h": 0, "j": 0, "o": 0}

        def mk_group(st, gi):
            j0, nj = groups[gi]
            par = cnt["g"] % 2
            cnt["g"] += 1
            ncols = nj * 128

            def load():
                p.add("sp", lambda e: e.dma_start(out=wgb[par][:, :, 0:ncols],
                                                  in_=dap(wg_bf, j0 * 128, [[DFF, 128], [128 * DFF, KC], [1, ncols]])),
                      writes=[("wgb", par)])
                p.add("sp", lambda e: e.dma_start(out=wub[par][:, :, 0:ncols],
                                                  in_=dap(wu_bf, j0 * 128, [[DFF, 128], [128 * DFF, KC], [1, ncols]])),
                      writes=[("wub", par)])

            def compute():
                for jj in range(nj):
                    j = j0 + jj
                    for hf in range(NH):
                        yT = io.yT[hf]
                        pj = cnt["j"] % 2
                        cnt["j"] += 1
                        for kc in range(KC):
                            p.add("pe", lambda e, kc=kc, jj=jj, pj=pj, yT=yT: e.matmul(
                                psg[pj][:], lhsT=wgb[par][:, kc, jj * 128:(jj + 1) * 128], rhs=yT[:, kc, :],
                                start=(kc == 0), stop=(kc == KC - 1)),
                                reads=[("wgb", par), ("yT", hf, kc)], writes=[("psg", pj)])
                        for kc in range(KC):
                            p.add("pe", lambda e, kc=kc, jj=jj, pj=pj, yT=yT: e.matmul(
                                psu[pj][:], lhsT=wub[par][:, kc, jj * 128:(jj + 1) * 128], rhs=yT[:, kc, :],
                                start=(kc == 0), stop=(kc == KC - 1)),
                                reads=[("wub", par), ("yT", hf, kc)], writes=[("psu", pj)])
                        p.add("act", lambda e, pj=pj: e.activation(out=sg[pj][:], in_=psg[pj][:], func=AF.Silu),
                              reads=[("psg", pj)], writes=[("sg", pj)])
                        p.add("dve", lambda e, pj=pj, j=j, hf=hf: e.tensor_tensor(out=a[:, j, hf, :], in0=sg[pj][:], in1=psu[pj][:], op=ALU.mult),
                              reads=[("sg", pj), ("psu", pj)], writes=[("a", j, hf)])
            return load, compute

        def mk_half(st, half):
            par = cnt["h"] % 2
            cnt["h"] += 1

            def load():
                p.add("sp", lambda e: e.dma_start(out=wdb[par][:],
                                                  in_=dap(wd_bf, half * 512, [[D, 128], [128 * D, NJ], [1, 512]])),
                      writes=[("wdb", par)])

            def compute():
                for dcl in range(4):
                    dc = half * 4 + dcl
                    for hf in range(NH):
                        po = cnt["o"] % 2
                        cnt["o"] += 1
                        for j in range(NJ):
                            p.add("pe", lambda e, j=j, dcl=dcl, po=po, hf=hf: e.matmul(
                                pso[po][:], lhsT=wdb[par][:, j, dcl * 128:(dcl + 1) * 128], rhs=a[:, j, hf, :],
                                start=(j == 0), stop=(j == NJ - 1)),
                                reads=[("wdb", par), ("a", j, hf)], writes=[("pso", po)])
                        io.residual(st * NH + hf, dc, pso[po], ("pso", po))
                if half == 1:
                    for hf in range(NH):
                        io.store(st * NH + hf)
            return load, compute

        emit_casts(p, g, list(extra_casts))
        nst = ntiles // NH
        for st in range(nst):
            for gi in range(len(groups)):
                tasks.append(("g", st, gi) + mk_group(st, gi))
            for half in range(2):
                tasks.append(("h", st, half) + mk_half(st, half))

        tasks[0][3]()
        for k, (kind, st, idx, load, compute) in enumerate(tasks):
            if kind == "g" and idx == 0:
                for hf in range(NH):
                    io.load(st * NH + hf)
                for hf in range(NH):
                    io.norm(st * NH + hf)
            if k + 1 < len(tasks):
                tasks[k + 1][3]()
            compute()
        p.emit()


def attn_consts(C, p, es, g):
    nc = C.nc
    sb = lambda n, s, dt: es.enter_context(nc.sbuf_tensor(C.name(n), list(s), dt))
    k = K()
    tmp = sb("actmp", [128, 128], F32)
    k.blockones = sb("blockones", [128, 128], BF16)
    k.negtri = sb("negtri", [128, 128], BF16)
    k.maskst = sb("maskst", [128, 128], BF16)
    k.negones = sb("negones", [128, 2], BF16)
    p.add("pool", lambda e: e.memset(tmp[:], 0.0), writes=["actmp"])
    p.add("pool", lambda e: e.memset(tmp[0:64, 0:64], 1.0), writes=["actmp"])
    p.add("pool", lambda e: e.memset(tmp[64:128, 64:128], 1.0), writes=["actmp"])
    p.add("dve", lambda e: e.tensor_copy(out=k.blockones[:], in_=tmp[:]), reads=["actmp"], writes=["blockones"])
    p.add("pool", lambda e: e.memset(tmp[:], -1.0), writes=["actmp"])
    p.add("pool", lambda e: e.affine_select(out=tmp[:], in_=tmp[:], pattern=[[-1, 128]], compare_op=ALU.is_ge, fill=0.0,
                                            base=0, channel_multiplier=1), reads=["actmp"], writes=["actmp"])
    p.add("dve", lambda e: e.tensor_copy(out=k.negtri[:], in_=tmp[:]), reads=["actmp"], writes=["negtri"])
    p.add("pool", lambda e: e.memset(tmp[:], 1.0), writes=["actmp"])
    p.add("pool", lambda e: e.affine_select(out=tmp[:], in_=tmp[:], pattern=[[1, 128]], compare_op=ALU.is_gt, fill=0.0,
                                            base=0, channel_multiplier=-1), reads=["actmp"], writes=["actmp"])
    p.add("dve", lambda e: e.tensor_copy(out=k.maskst[:], in_=tmp[:]), reads=["actmp"], writes=["maskst"])
    p.add("dve", lambda e: e.memset(k.negones[:], -1.0), writes=["negones"])
    return k


def attn_qkv_pass(C, g, cs, l, src, ntiles=NT):
    nc = C.nc
    p = Prog(C)
    with contextlib.ExitStack() as es:
        sb = lambda n, s, dt: es.enter_context(nc.sbuf_tensor(C.name(n), list(s), dt))
        ps = lambda n: es.enter_context(nc.psum_tensor(C.name(n), [128, 512], F32))
        io = TileIO(C, p, es, g, cs, "hT", src, None, l, 1)
        ac = attn_consts(C, p, es, g)
        w = sb("wqkv", [128, KC, 3 * D], BF16)
        gqk = sb("gqk", [128, 2], F32)
        sq2 = [sb("sq2", [128, T], BF16) for _ in range(2)]
        r2 = [sb("r2", [128, T], F32) for _ in range(2)]
        stg = [sb("stg", [128, T], BF16) for _ in range(2)]
        vst = [sb("vst", [128, 4, D], BF16) for _ in range(2)]
        psqk = [ps("psqk") for _ in range(3)]
        psn2 = [ps("psn2") for _ in range(2)]
        psv = [ps("psv") for _ in range(2)]
        for kc in range(KC):
            p.add("sp", lambda e, kc=kc: e.dma_start(out=w[:, kc, :], in_=dap(g.sb_wqkv_bf, kc * 128 * 3 * D, [[3 * D, 128], [1, 3 * D]])),
                  writes=[("w", kc)])
        for half in range(2):
            p.add("sp", lambda e, half=half: e.dma_start(out=gqk[half * 64:(half + 1) * 64, 0:1], in_=g.sb_qg.ap()), writes=["gqk"])
            p.add("sp", lambda e, half=half: e.dma_start(out=gqk[half * 64:(half + 1) * 64, 1:2], in_=g.sb_kg.ap()), writes=["gqk"])
        p.add("dve", lambda e: e.tensor_scalar(out=gqk[:, 0:1], in0=gqk[:, 0:1], scalar1=0.125, scalar2=None, op0=ALU.mult),
              reads=["gqk"], writes=["gqk"])
        cnt = [0, 0]

        def qk_unit(i, hp, which):
            b = i % io.nbuf
            yT = io.yT[b]
            u = cnt[0]
            cnt[0] += 1
            par = u % 2
            p3 = u % 3
            col0 = which * D + hp * 128
            for kc in range(KC):
                p.add("pe", lambda e, kc=kc: e.matmul(psqk[p3][:], lhsT=w[:, kc, col0:col0 + 128], rhs=yT[:, kc, :],
                                                      start=(kc == 0), stop=(kc == KC - 1)),
                      reads=[("w", kc), ("yT", b, kc)], writes=[("psqk", p3)])
            p.add("act", lambda e: e.activation(out=sq2[par][:], in_=psqk[p3][:], func=AF.Square),
                  reads=[("psqk", p3)], writes=[("sq2", par)])
            p.add("pe", lambda e: e.matmul(psn2[par][:], lhsT=ac.blockones[:], rhs=sq2[par][:], start=True, stop=True),
                  reads=[("sq2", par), "blockones"], writes=[("psn2", par)])
            p.add("act", lambda e: e.activation(out=r2[par][:], in_=psn2[par][:], func=AF.Ln, scale=1.0 / 64, bias=EPS),
                  reads=[("psn2", par)], writes=[("r2", par)])
            p.add("act", lambda e: e.activation(out=r2[par][:], in_=r2[par][:], func=AF.Exp, scale=-0.5),
                  reads=[("r2", par)], writes=[("r2", par)])
            p.add("dve", lambda e: e.scalar_tensor_tensor(out=stg[par][:], in0=psqk[p3][:], scalar=gqk[:, which:which + 1],
                                                          in1=r2[par][:], op0=ALU.mult, op1=ALU.mult),
                  reads=[("psqk", p3), ("r2", par), "gqk"], writes=[("stg", par)])
            dst = g.qT if which == 0 else g.kT
            p.add("aq", lambda e: e.dma_start(out=dap(dst, hp * 128 * S + i * T, [[S, 128], [1, T]]), in_=stg[par][:]),
                  reads=[("stg", par)])

        def v_units(i):
            b = i % io.nbuf
            yT = io.yT[b]
            vt = vst[i % 2]
            for s4 in range(4):
                for half in range(2):
                    u = cnt[1]
                    cnt[1] += 1
                    par = u % 2
                    for kc in range(KC):
                        p.add("pe", lambda e, kc=kc, s4=s4, half=half, par=par: e.matmul(
                            psv[par][:], lhsT=yT[:, kc, s4 * 128:(s4 + 1) * 128],
                            rhs=w[:, kc, 2 * D + half * 512:2 * D + (half + 1) * 512], start=(kc == 0), stop=(kc == KC - 1)),
                            reads=[("w", kc), ("yT", b, kc)], writes=[("psv", par)])
                    if u % 2 == 0:
                        p.add("act", lambda e, s4=s4, half=half, par=par: e.activation(
                            out=vt[:, s4, half * 512:(half + 1) * 512], in_=psv[par][:], func=AF.Copy),
                            reads=[("psv", par)], writes=[("vst", i % 2, s4, half)])
                    else:
                        p.add("dve", lambda e, s4=s4, half=half, par=par: e.tensor_copy(
                            out=vt[:, s4, half * 512:(half + 1) * 512], in_=psv[par][:]),
                            reads=[("psv", par)], writes=[("vst", i % 2, s4, half)])
            p.add("aq", lambda e: e.dma_start(out=dap(g.v_s, i * T * D, [[D, 128], [128 * D, 4], [1, D]]), in_=vt[:]),
                  reads=[("vst", i % 2, s4, half) for s4 in range(4) for half in range(2)])

        io.load(0)
        io.norm(0)
        for i in range(ntiles):
            if i + 1 < ntiles:
                io.load(i + 1)
            for hp in range(8):
                qk_unit(i, hp, 0)
                qk_unit(i, hp, 1)
                if hp == 3 and i + 1 < ntiles:
                    io.norm(i + 1)
            v_units(i)
        p.emit()


def attn_core_pass(C, g, cs, ntiles=NT):
    nc = C.nc
    p = Prog(C)
    Se = ntiles * T
    nkb_tot = Se // 128
    with contextlib.ExitStack() as es:
        sb = lambda n, s, dt: es.enter_context(nc.sbuf_tensor(C.name(n), list(s), dt))
        ps = lambda n: es.enter_context(nc.psum_tensor(C.name(n), [128, 512], F32))
        ac = attn_consts(C, p, es, g)
        qt = [sb("qt", [128, Se], BF16) for _ in range(2)]
        kt = [sb("kt", [128, Se], BF16) for _ in range(2)]
        vv = [sb("vv", [128, nkb_tot, 128], BF16) for _ in range(2)]
        e1 = [sb("e1", [128, T], F32) for _ in range(2)]
        spm = [sb("spm", [128, T], BF16) for _ in range(2)]
        wl = [sb("wl", [128, T], BF16) for _ in range(2)]
        gg = [sb("gg", [128, 4], F32) for _ in range(2)]
        acc = [sb("acc", [128, 4, 64], F32) for _ in range(2)]
        oc = [sb("oc", [128, 4, 128], BF16) for _ in range(2)]
        psz = [ps("psz") for _ in range(2)]
        pse = [ps("pse") for _ in range(2)]
        psp = [ps("psp") for _ in range(2)]
        cnt = [0, 0]

        def load_pair(hp):
            b = hp % 2
            p.add("sp", lambda e: e.dma_start(out=qt[b][:], in_=dap(g.qT, hp * 128 * S, [[S, 128], [1, Se]])), writes=[("qt", b)])
            p.add("sp", lambda e: e.dma_start(out=kt[b][:], in_=dap(g.kT, hp * 128 * S, [[S, 128], [1, Se]])), writes=[("kt", b)])
            p.add("sp", lambda e: e.dma_start(out=vv[b][:], in_=dap(g.v_s, hp * 128, [[D, 128], [128 * D, nkb_tot], [1, 128]])),
                  writes=[("vv", b)])

        spm3 = spm + [sb("spm", [128, T], BF16)]

        class St:
            pass

        def mk(hp, hh, qc, kb, ab, u, last):
            st = St()
            st.hp, st.hh, st.qc, st.kb, st.ab, st.u, st.last = hp, hh, qc, kb, ab, u, last
            st.b = hp % 2
            st.pb = hh * 64
            r = kb - 4 * qc
            st.qb0 = max(r, 0)
            st.c0 = st.qb0 * 128
            st.diag = r >= 0
            st.ksl = sap(kt[st.b], kb * 128, [[1, 128]], p0=st.pb, pn=64)
            st.qsl = sap(qt[st.b], qc * T + st.c0, [[1, T - st.c0]], p0=st.pb, pn=64)
            return st

        def stageA(st):
            b, c0, u = st.b, st.c0, st.u
            par, p3 = u % 2, u % 3
            Z = psz[par]
            p.add("pe", lambda e: e.matmul(Z[:, c0:T], lhsT=st.ksl, rhs=st.qsl, start=True, stop=True),
                  reads=[("kt", b), ("qt", b)], writes=[("psz", par)])
            p.add("act", lambda e: e.activation(out=e1[par][:, c0:T], in_=Z[:, c0:T], func=AF.Exp),
                  reads=[("psz", par)], writes=[("e1", par)])
            p.add("act", lambda e: e.activation(out=spm3[p3][:, c0:T], in_=e1[par][:, c0:T], func=AF.Ln, bias=1.0),
                  reads=[("e1", par)], writes=[("spm", p3)])
            if st.diag:
                p.add("dve", lambda e: e.tensor_tensor(out=spm3[p3][:, c0:c0 + 128], in0=spm3[p3][:, c0:c0 + 128],
                                                        in1=ac.maskst[:], op=ALU.mult),
                      reads=[("spm", p3), "maskst"], writes=[("spm", p3)])

        def stageB1(st):
            b, c0, u = st.b, st.c0, st.u
            par, p3 = u % 2, u % 3
            E = pse[par]
            p.add("pe", lambda e: e.matmul(E[:, c0:T], lhsT=st.ksl, rhs=st.qsl, start=True, stop=False),
                  reads=[("kt", b), ("qt", b)], writes=[("pse", par)])
            p.add("pe", lambda e: e.matmul(E[:, c0:T], lhsT=ac.negtri[:], rhs=spm3[p3][:, c0:T], start=False, stop=True),
                  reads=[("spm", p3), "negtri"], writes=[("pse", par)])
            p.add("act", lambda e: e.activation(out=wl[par][:, c0:T], in_=E[:, c0:T], func=AF.Exp),
                  reads=[("pse", par)], writes=[("wl", par)])
            if st.diag:
                p.add("dve", lambda e: e.tensor_tensor(out=wl[par][:, c0:c0 + 128], in0=wl[par][:, c0:c0 + 128],
                                                        in1=ac.maskst[:], op=ALU.mult),
                      reads=[("wl", par), "maskst"], writes=[("wl", par)])

        def stageB2(st):
            b, c0, u, kb, hh, ab, qb0 = st.b, st.c0, st.u, st.kb, st.hh, st.ab, st.qb0
            par, p3 = u % 2, u % 3
            P = psp[par]
            for qb in range(qb0, 4):
                p.add("pe", lambda e, qb=qb: e.matmul(P[:, 256 + qb:256 + qb + 1], lhsT=spm3[p3][:, qb * 128:(qb + 1) * 128],
                                                      rhs=ac.negones[:, 0:1], start=True, stop=True),
                      reads=[("spm", p3), "negones"], writes=[("psp", par)])
            for qb in range(qb0, 4):
                p.add("pe", lambda e, qb=qb: e.matmul(P[:, qb * 64:(qb + 1) * 64], lhsT=wl[par][:, qb * 128:(qb + 1) * 128],
                                                      rhs=vv[b][:, kb, hh * 64:(hh + 1) * 64], start=True, stop=True),
                      reads=[("wl", par), ("vv", b)], writes=[("psp", par)])
            if kb > 0:
                p.add("act", lambda e: e.activation(out=gg[par][:, qb0:4], in_=P[:, 256 + qb0:260], func=AF.Exp),
                      reads=[("psp", par)], writes=[("gg", par)])
            A = acc[ab]
            for qb in range(qb0, 4):
                if kb == 0:
                    p.add("dve", lambda e, qb=qb: e.tensor_copy(out=A[:, qb, :], in_=P[:, qb * 64:(qb + 1) * 64]),
                          reads=[("psp", par)], writes=[("acc", ab, qb)])
                else:
                    p.add("dve", lambda e, qb=qb: e.scalar_tensor_tensor(
                        out=A[:, qb, :], in0=A[:, qb, :], scalar=gg[par][:, qb:qb + 1], in1=P[:, qb * 64:(qb + 1) * 64],
                        op0=ALU.mult, op1=ALU.add),
                        reads=[("psp", par), ("gg", par), ("acc", ab, qb)], writes=[("acc", ab, qb)])
            if st.last:
                hp, qc = st.hp, st.qc
                ob = (hp * ntiles + qc) % 2
                O = oc[ob]
                p.add("dve", lambda e: e.tensor_copy(out=O[:, :, hh * 64:(hh + 1) * 64], in_=acc[ab][:]),
                      reads=[("acc", ab, qb) for qb in range(4)], writes=[("oc", ob, hh)])
                if hh == 1:
                    p.add("aq", lambda e: e.dma_start(out=dap(g.o_s, qc * T * D + hp * 128, [[D, 128], [128 * D, 4], [1, 128]]), in_=O[:]),
                          reads=[("oc", ob, 0), ("oc", ob, 1)])

        steps = []
        nchunk = 0
        for hp in range(8):
            for qc in range(ntiles):
                for hh in range(2):
                    nk = 4 * qc + 4
                    for kb in range(nk):
                        steps.append(mk(hp, hh, qc, kb, nchunk % 2, len(steps), kb == nk - 1))
                    nchunk += 1
        N = len(steps)
        loaded = set()
        for n in range(N + 2):
            if n < N:
                st = steps[n]
                if st.hp not in loaded:
                    load_pair(st.hp)
                    loaded.add(st.hp)
                stageA(st)
            if 0 <= n - 1 < N:
                stageB1(steps[n - 1])
            if 0 <= n - 2 < N:
                stageB2(steps[n - 2])
        p.emit()


def attn_out_pass(C, g, cs, l, src, dst, ntiles=NT):
    nc = C.nc
    p = Prog(C)
    with contextlib.ExitStack() as es:
        sb = lambda n, s, dt: es.enter_context(nc.sbuf_tensor(C.name(n), list(s), dt))
        ps = lambda n: es.enter_context(nc.psum_tensor(C.name(n), [128, 512], F32))
        io = TileIO(C, p, es, g, cs, "hT", src, dst, l, 1)
        wo = sb("wo", [128, KC, D], BF16)
        ot = [sb("otk", [128, 4, D], BF16) for _ in range(2)]
        oT = [sb("oT", [128, KC, T], BF16) for _ in range(2)]
        pst = [es.enter_context(nc.psum_tensor(C.name("pstb"), [128, 1024], BF16)) for _ in range(2)]
        pso = [ps("pso") for _ in range(2)]
        for kc in range(KC):
            p.add("sp", lambda e, kc=kc: e.dma_start(out=wo[:, kc, :], in_=dap(g.sb_wo_bf, kc * 128 * D, [[D, 128], [1, D]])),
                  writes=[("wo", kc)])
        cnt = [0, 0]

        def tile(i):
            b = i % 2
            p.add("sp", lambda e: e.dma_start(out=ot[b][:], in_=dap(g.o_s, i * T * D, [[D, 128], [128 * D, 4], [1, D]])),
                  writes=[("ot", b)])
            io.load(i)
            for kc in range(KC):
                u = cnt[0]
                cnt[0] += 1
                par = u % 2
                for s4 in range(4):
                    p.add("pe", lambda e, kc=kc, s4=s4, par=par: e.transpose(
                        pst[par][:, s4 * 128:(s4 + 1) * 128], ot[b][:, s4, kc * 128:(kc + 1) * 128], cs.ident_bf[:]),
                        reads=[("ot", b), "ident_bf"], writes=[("pst", par)])
                if u % 2 == 0:
                    p.add("act", lambda e, kc=kc, par=par: e.activation(out=oT[b][:, kc, :], in_=pst[par][:, 0:T], func=AF.Copy),
                          reads=[("pst", par)], writes=[("oT", b, kc)])
                else:
                    p.add("dve", lambda e, kc=kc, par=par: e.tensor_copy(out=oT[b][:, kc, :], in_=pst[par][:, 0:T]),
                          reads=[("pst", par)], writes=[("oT", b, kc)])
            for dc in range(KC):
                po = cnt[1] % 2
                cnt[1] += 1
                for kc in range(KC):
                    p.add("pe", lambda e, kc=kc, dc=dc, po=po: e.matmul(
                        pso[po][:], lhsT=wo[:, kc, dc * 128:(dc + 1) * 128], rhs=oT[b][:, kc, :],
                        start=(kc == 0), stop=(kc == KC - 1)),
                        reads=[("wo", kc), ("oT", b, kc)], writes=[("pso", po)])
                io.residual(i, dc, pso[po], ("pso", po))
            io.store(i)
        for i in range(ntiles):
            tile(i)
        p.emit()


def hybrid_pass(C, g, cs, l, src, dst, ntiles=NT, extra_casts=()):
    nc = C.nc
    p = Prog(C)
    with contextlib.ExitStack() as es:
        sb = lambda n, s, dt: es.enter_context(nc.sbuf_tensor(C.name(n), list(s), dt))
        ps = lambda n, w=512, dt=F32: es.enter_context(nc.psum_tensor(C.name(n), [128, w], dt))
        io = TileIO(C, p, es, g, cs, "hT", src, dst, l, 1, nbuf=1)
        tri = sb("tri", [128, 128], F32)
        onesf = sb("onesf", [128, 128], F32)
        a_bc = sb("a_bc", [128, 16], F32)
        dtb_bc = sb("dtb_bc", [128, 16], F32)
        d_bc = sb("d_bc", [128, 16], F32)
        dmat = sb("dmat", [128, 16, 128], BF16)
        vngb = sb("vngb", [128, D], F32)
        ngb = sb("ngb", [128, D], F32)
        browf = sb("browf", [1, D], F32)
        brow = sb("brow", [1, D], BF16)
        onesrow = sb("onesrow", [1, 128], BF16)
        wmT = sb("wmT", [128, 8, 128], BF16)
        convw = sb("convw", [128, 48], F32)
        convb = sb("convb", [128, 12], F32)
        wbuf = [sb("wbuf", [128, KC, 512], BF16) for _ in range(2)]
        wob = sb("wob", [128, 16, 256], BF16)
        uT = sb("uT", [128, 8, T], BF16)
        vg = sb("vg", [128, 4, D], BF16)
        zs = sb("zs", [128, 4, D], BF16)
        xr = [sb("xr", [128, T + 3], F32) for _ in range(2)]
        tail = sb("tail", [128, 12, 3], F32)
        cacc = [sb("cacc", [128, T], F32) for _ in range(2)]
        xc = sb("xc", [128, 8, T], BF16)
        bT = sb("bT", [128, 2, T], BF16)
        cT = sb("cT", [128, 2, T], BF16)
        yab = sb("yab", [128, 16, T], BF16)
        dtt = sb("dtt", [128, 4, 16], F32)
        dA = sb("dA", [128, 4, 16], F32)
        junk = sb("junk", [128, 8, 128], BF16)
        ssv = sb("ssv", [128, 8], F32)
        vtmp = sb("vtmp", [128, D], F32)
        wsraw = vtmp
        vn = sb("vn", [128, 8, 128], BF16)
        xs = sb("xs", [128, D], BF16)
        xdt = sb("xdt", [128, D], BF16)
        xdd = sb("xdd", [128, D], BF16)
        btok = sb("btok", [128, 2, 128], BF16)
        rhsA = sb("rhsA", [128, 8, 128], F32)
        cst = sb("cst", [128, 16], F32)
        seg = [sb("seg", [128, 8, 128], F32) for _ in range(2)]
        eb = [sb("eb", [128, 8, 128], F32) for _ in range(2)]
        mT = sb("mT", [128, 16, 128], BF16)
        csm = sb("csm", [128, 16, 128], BF16)
        cbm = sb("cbm", [128, 2, 128], F32)
        yz = sb("yz", [128, D], F32)
        ssy = sb("ssy", [128, 2], F32)
        yb = sb("yb", [128, D], BF16)
        hin = sb("hin", [128, D], F32)
        hbf = sb("hbf", [128, D], BF16)
        psP = [ps("psP") for _ in range(2)]
        psC = ps("psC", 1024)
        psY = ps("psY", 1024)
        psM = ps("psM")
        cnt = [0, 0]

        regcache = {}

        def fillreg(e):
            if "r" not in regcache:
                regcache["r"] = e.to_reg(-30000.0)
            return regcache["r"]

        def P_():
            u = cnt[0]
            cnt[0] += 1
            return psP[u % 2], ("psP", u % 2)

        def Pb():
            pt, pk = P_()
            return pt.bitcast(BF16), pk

        p.add("pool", lambda e: e.memset(tri[:], 1.0), writes=["tri"])
        DEFER = list(extra_casts)
        p.add("pool", lambda e: e.affine_select(out=tri[:], in_=tri[:], pattern=[[1, 128]], compare_op=ALU.is_ge, fill=0.0,
                                                base=0, channel_multiplier=-1), reads=["tri"], writes=["tri"])
        p.add("pool", lambda e: e.memset(onesf[:], 1.0), writes=["onesf"])
        p.add("dve", lambda e: e.memset(onesrow[:], 1.0), writes=["onesrow"])
        p.add("dve", lambda e: e.memset(hin[:], 0.0), writes=["hin"])
        p.add("dve", lambda e: e.memset(hbf[:], 0.0), writes=["hbf"])
        p.add("dve", lambda e: e.memset(tail[:], 0.0), writes=["tail"])
        bc = lambda t, n: dap(t, 0, [[0, 128], [1, n]])
        p.add("sp", lambda e: e.dma_start(out=a_bc[:], in_=bc(g.a_log, 16)), writes=["a_bc"])
        p.add("sp", lambda e: e.dma_start(out=dtb_bc[:], in_=bc(g.dt_bias, 16)), writes=["dtb_bc"])
        p.add("sp", lambda e: e.dma_start(out=d_bc[:], in_=bc(g.ssd_d, 16)), writes=["d_bc"])
        p.add("sp", lambda e: e.dma_start(out=vngb[:], in_=bc(g.gm_vg, D)), writes=["vngb"])
        p.add("sp", lambda e: e.dma_start(out=ngb[:], in_=bc(g.ssd_ng, D)), writes=["ngb"])
        p.add("sp", lambda e: e.dma_start(out=browf[:], in_=g.gm_bs.ap()), writes=["browf"])
        p.add("sp", lambda e: e.dma_start(out=convw[:], in_=g.conv_wT.ap()), writes=["convw"])
        p.add("sp", lambda e: e.dma_start(out=convb[:], in_=g.conv_bT.ap()), writes=["convb"])
        p.add("sp", lambda e: e.dma_start(out=sap(wsraw, 0, [[128, 8], [1, 128]]), in_=dap(g.gm_ws, 0, [[128, 128], [128 * 128, 8], [1, 128]])), writes=["wsraw", "vtmp"])
        p.add("dve", lambda e: e.tensor_copy(out=brow[:], in_=browf[:]), reads=["browf"], writes=["brow"])
        p.add("act", lambda e: e.activation(out=a_bc[:], in_=a_bc[:], func=AF.Exp), reads=["a_bc"], writes=["a_bc"])
        p.add("dve", lambda e: e.tensor_scalar(out=a_bc[:], in0=a_bc[:], scalar1=-1.0, scalar2=None, op0=ALU.mult),
              reads=["a_bc"], writes=["a_bc"])
        p.add("dve", lambda e: e.tensor_tensor(out=dmat[:], in0=sap(cs.ident, 0, [[0, 16], [1, 128]]),
                                               in1=sap(d_bc, 0, [[1, 16], [0, 128]]), op=ALU.mult),
              reads=["d_bc"], writes=["dmat"])
        p.add("dve", lambda e: e.memset(sap(wsraw, 64, [[128, 8], [1, 64]], p0=0, pn=64), 0.0), reads=["wsraw"], writes=["wsraw"])
        for gi in range(8):
            pt, pk = P_()
            p.add("pe", lambda e, gi=gi, pt=pt: e.transpose(pt[:, 0:128], wsraw[:, gi * 128:(gi + 1) * 128], cs.ident[:]), reads=["wsraw", "vtmp"], writes=[pk])
            p.add("dve", lambda e, gi=gi, pt=pt: e.tensor_copy(out=wmT[:, gi, :], in_=pt[:, 0:128]), reads=[pk], writes=["wmT"])

        deferred = []
        emit_casts(p, g, DEFER, collect=deferred)
        per_blk = (len(deferred) + 4 * ntiles - 1) // (4 * ntiles)

        def load_w(c0, ncols, par):
            p.add("sp", lambda e: e.dma_start(out=wbuf[par][:, :, 0:ncols],
                                              in_=dap(g.hy_w_in_bf, c0, [[IN_W, 128], [128 * IN_W, KC], [1, ncols]])),
                  writes=[("wbuf", par)])

        def featmajor_group(yT, par, jj):
            pt, pk = P_()
            for kc in range(KC):
                p.add("pe", lambda e, kc=kc: e.matmul(pt[:], lhsT=wbuf[par][:, kc, jj * 128:(jj + 1) * 128], rhs=yT[:, kc, :],
                                                      start=(kc == 0), stop=(kc == KC - 1)),
                      reads=[("wbuf", par), ("yT", 0, kc)], writes=[pk])
            return pt, pk

        def tokmajor_group(yT, par, s4, ncols=512):
            pt, pk = P_()
            for kc in range(KC):
                p.add("pe", lambda e, kc=kc: e.matmul(pt[:, 0:ncols], lhsT=yT[:, kc, s4 * 128:(s4 + 1) * 128], rhs=wbuf[par][:, kc, 0:ncols],
                                                      start=(kc == 0), stop=(kc == KC - 1)),
                      reads=[("wbuf", par), ("yT", 0, kc)], writes=[pk])
            return pt, pk

        def conv_chunk(ch, pt, pk):
            par = ch % 2
            X, A = xr[par], cacc[par]
            xk, ak = ("xr", par), ("cacc", par)
            p.add("act", lambda e: e.activation(out=X[:, 3:T + 3], in_=pt[:], func=AF.Copy), reads=[pk], writes=[xk])
            te = "pool"
            p.add(te, lambda e: e.tensor_copy(out=X[:, 0:3], in_=tail[:, ch, :]), reads=[("tail", ch)], writes=[xk])
            p.add(te, lambda e: e.tensor_copy(out=tail[:, ch, :], in_=X[:, T:T + 3]), reads=[xk], writes=[("tail", ch)])
            p.add("dve", lambda e: e.tensor_scalar(out=A[:], in0=X[:, 3:T + 3], scalar1=convw[:, ch * 4 + 3:ch * 4 + 4], scalar2=None,
                                                   op0=ALU.mult), reads=[xk, "convw"], writes=[ak])
            for tap in range(3):
                p.add("dve", lambda e, tap=tap: e.scalar_tensor_tensor(
                    out=A[:], in0=X[:, tap:tap + T], scalar=convw[:, ch * 4 + tap:ch * 4 + tap + 1], in1=A[:],
                    op0=ALU.mult, op1=ALU.add), reads=[xk, ak, "convw"], writes=[ak])
            if ch < 8:
                dst_ap, dk = xc[:, ch, :], ("xc", ch)
            elif ch < 10:
                dst_ap, dk = bT[:, ch - 8, :], ("bT", ch - 8)
            else:
                dst_ap, dk = cT[:, ch - 10, :], ("cT", ch - 10)
            p.add("act", lambda e: e.activation(out=dst_ap, in_=A[:], func=AF.Silu, bias=convb[:, ch:ch + 1]),
                  reads=[ak, "convb"], writes=[dk])

        def gmlp_block(s4):
            blk = slice(s4 * 128, (s4 + 1) * 128)
            for gi in range(8):
                p.add("act", lambda e, gi=gi: e.activation(out=junk[:, gi, :], in_=vg[:, s4, gi * 128:(gi + 1) * 128], func=AF.Square,
                                                           accum_out=ssv[:, gi:gi + 1]),
                      reads=[("vg", s4)], writes=[("ssv", gi), ("junk", gi)])
            yield
            p.add("act", lambda e: e.activation(out=ssv[:], in_=ssv[:], func=AF.Ln, scale=1.0 / 128, bias=EPS), reads=[("ssv", gi) for gi in range(8)], writes=["ssv"])
            p.add("act", lambda e: e.activation(out=ssv[:], in_=ssv[:], func=AF.Exp, scale=-0.5), reads=["ssv"], writes=["ssv"] + [("ssv", gi) for gi in range(8)])
            p.add("dve", lambda e: e.tensor_tensor(out=vtmp[:], in0=vg[:, s4, :], in1=sap(ssv, 0, [[1, 8], [0, 128]]), op=ALU.mult),
                  reads=[("vg", s4), "ssv"], writes=["vtmp"])
            yield
            p.add("dve", lambda e: e.tensor_tensor(out=vn[:], in0=vtmp[:], in1=vngb[:], op=ALU.mult),
                  reads=["vtmp", "vngb"], writes=["vn"])
            yield
            for half in range(2):
                if half == 1:
                    yield
                pt, pk = P_()
                for q in range(4):
                    gi = half * 4 + q
                    p.add("pe", lambda e, gi=gi, q=q, pt=pt: e.matmul(pt[:, q * 128:(q + 1) * 128], lhsT=vn[:, gi, :], rhs=wmT[:, gi, :],
                                                               start=True, stop=False), reads=["vn", "wmT"], writes=[pk])
                    p.add("pe", lambda e, gi=gi, q=q, pt=pt: e.matmul(pt[:, q * 128:(q + 1) * 128], lhsT=onesrow[:], rhs=brow[:, gi * 128:(gi + 1) * 128],
                                                               start=False, stop=True), reads=["onesrow", "brow"], writes=[pk])
                p.add("dve", lambda e, half=half, pt=pt: e.tensor_tensor(
                    out=yab[:, half * 4:half * 4 + 4, blk], in0=uT[:, half * 4:half * 4 + 4, blk],
                    in1=sap(pt, 0, [[128, 4], [1, 128]]), op=ALU.mult),
                    reads=[pk] + [("uT", half * 4 + q) for q in range(4)], writes=[("yab", half * 4 + q) for q in range(4)])

        def ssd_chunk(s4, first):
            blk = slice(s4 * 128, (s4 + 1) * 128)
            psTb, tk = Pb()
            for kc in range(8):
                p.add("pe", lambda e, kc=kc: e.transpose(psTb[:, kc * 128:(kc + 1) * 128], xc[:, kc, blk], cs.ident_bf[:]),
                      reads=[("xc", kc)], writes=[tk])
            p.add("act", lambda e: e.activation(out=xs[:], in_=psTb[:], func=AF.Copy), reads=[tk], writes=["xs"])
            p.add("dve", lambda e: e.tensor_tensor(out=xdt[:], in0=psTb[:], in1=sap(dtt, s4 * 16, [[1, 16], [0, 64]]), op=ALU.mult),
                  reads=[tk, ("dtt", s4)], writes=["xdt"])
            yield
            psTb2, tk2 = Pb()
            for gi in range(2):
                p.add("pe", lambda e, gi=gi: e.transpose(psTb2[:, gi * 128:(gi + 1) * 128], bT[:, gi, blk], cs.ident_bf[:]),
                      reads=[("bT", gi)], writes=[tk2])
            p.add("act", lambda e: e.activation(out=btok[:], in_=psTb2[:, 0:256], func=AF.Copy), reads=[tk2], writes=["btok"])
            yield
            for gi in range(2):
                p.add("pe", lambda e, gi=gi: e.matmul(psM[:, gi * 128:(gi + 1) * 128], lhsT=bT[:, gi, blk], rhs=cT[:, gi, blk],
                                                      start=True, stop=True), reads=[("bT", gi), ("cT", gi)], writes=["psM"])
            p.add("pe", lambda e: e.matmul(psM[:, 256:272], lhsT=tri[:], rhs=dA[:, s4, :], start=True, stop=True),
                  reads=["tri", ("dA", s4)], writes=["psM"])
            p.add("dve", lambda e: e.tensor_tensor(out=cbm[:], in0=psM[:, 0:256], in1=sap(tri, 0, [[0, 2], [1, 128]]), op=ALU.mult),
                  reads=["psM", "tri"], writes=["cbm"])
            p.add("act", lambda e: e.activation(out=cst[:], in_=psM[:, 256:272], func=AF.Copy), reads=["psM"], writes=["cst"])
            for gi in range(2):
                yield
                S_, E_ = seg[gi], eb[gi]
                sk, ek = ("seg", gi), ("eb", gi)
                p.add("pool" if "rhsa_pool" in SKIP else "dve", lambda e, gi=gi: e.tensor_tensor(out=rhsA[:], in0=sap(dA, s4 * 16 + gi * 8, [[1, 8], [0, 128]]),
                                                               in1=sap(tri, 0, [[0, 8], [1, 128]]), op=ALU.mult),
                      reads=[("dA", s4), "tri"], writes=["rhsA"])
                for q in range(2):
                    p.add("pe", lambda e, q=q: e.matmul(psC[:, q * 512:(q + 1) * 512], lhsT=onesf[:],
                                                        rhs=sap(rhsA, q * 512, [[1, 512]]), start=True, stop=True),
                          reads=["rhsA", "onesf"], writes=["psC"])
                yield
                p.add("dve", lambda e, gi=gi, S_=S_: e.tensor_tensor(out=S_[:], in0=sap(psC, 0, [[128, 8], [1, 128]]),
                                                                     in1=sap(cst, gi * 8, [[1, 8], [0, 128]]), op=ALU.subtract),
                      reads=["psC", "cst"], writes=[sk])
                if "min_pool" in SKIP:
                    p.add("pool", lambda e, S_=S_: e.affine_select(out=S_[:], in_=S_[:], pattern=[[0, 8], [1, 128]], compare_op=ALU.is_ge,
                                                                   fill=fillreg(e), base=0, channel_multiplier=-1), reads=[sk], writes=[sk])
                else:
                    p.add("dve", lambda e, S_=S_: e.tensor_scalar(out=S_[:], in0=S_[:], scalar1=0.0, scalar2=None, op0=ALU.min),
                          reads=[sk], writes=[sk])
                p.add("act", lambda e, S_=S_: e.activation(out=S_[:], in_=S_[:], func=AF.Exp), reads=[sk], writes=[sk])
                p.add("act", lambda e, E_=E_: e.activation(out=E_[:], in_=sap(psC, 0, [[128, 8], [1, 128]]), func=AF.Exp),
                      reads=["psC"], writes=[ek])
                yield
                p.add("dve", lambda e, gi=gi, S_=S_: e.tensor_tensor(out=mT[:, gi * 8:(gi + 1) * 8, :], in0=S_[:],
                                                                     in1=sap(cbm, gi * 128, [[0, 8], [1, 128]]), op=ALU.mult),
                      reads=[sk, "cbm"], writes=[("mT", gi)])
                if not first:
                    p.add("dve", lambda e, gi=gi, E_=E_: e.tensor_tensor(out=csm[:, gi * 8:(gi + 1) * 8, :], in0=E_[:],
                                                                         in1=sap(cT, gi * T + s4 * 128, [[0, 8], [1, 128]]), op=ALU.mult),
                          reads=[ek, ("cT", gi)], writes=[("csm", gi)])
            yield
            for h in range(16):
                gi = h // 8
                cols = slice(h * 64, (h + 1) * 64)
                p.add("pe", lambda e, h=h, cols=cols: e.matmul(psY[:, cols], lhsT=mT[:, h, :], rhs=xdt[:, cols], start=True, stop=False),
                      reads=[("mT", gi), "xdt"], writes=["psY"])
                p.add("pe", lambda e, h=h, cols=cols: e.matmul(psY[:, cols], lhsT=dmat[:, h, :], rhs=xs[:, cols], start=False, stop=first),
                      reads=["dmat", "xs"], writes=["psY"])
                if not first:
                    p.add("pe", lambda e, h=h, cols=cols: e.matmul(psY[:, cols], lhsT=csm[:, h, :], rhs=hbf[:, cols], start=False, stop=True),
                          reads=[("csm", gi), "hbf"], writes=["psY"])
            yield
            p.add("dve", lambda e: e.tensor_tensor(out=yz[:], in0=psY[:], in1=zs[:, s4, :], op=ALU.mult),
                  reads=["psY", ("zs", s4)], writes=["yz"])
            p.add("act", lambda e: e.activation(out=yb[:], in_=yz[:], func=AF.Square, accum_out=ssy[:, 0:1]),
                  reads=["yz"], writes=["ssy", "yb"])
            yield
            p.add("act", lambda e: e.activation(out=ssy[:, 1:2], in_=ssy[:, 0:1], func=AF.Ln, scale=1.0 / D, bias=EPS), reads=["ssy"], writes=["ssy"])
            p.add("act", lambda e: e.activation(out=ssy[:, 1:2], in_=ssy[:, 1:2], func=AF.Exp, scale=-0.5), reads=["ssy"], writes=["ssy"])
            p.add("dve", lambda e: e.scalar_tensor_tensor(out=yb[:], in0=yz[:], scalar=ssy[:, 1:2], in1=ngb[:], op0=ALU.mult, op1=ALU.mult),
                  reads=["yz", "ssy", "ngb"], writes=["yb"])
            yield
            psTb3, tk3 = Pb()
            for kc in range(8):
                p.add("pe", lambda e, kc=kc: e.transpose(psTb3[:, kc * 128:(kc + 1) * 128], yb[:, kc * 128:(kc + 1) * 128], cs.ident_bf[:]),
                      reads=["yb"], writes=[tk3])
            p.add("act", lambda e: e.activation(out=yab[:, 8:16, blk], in_=sap(psTb3, 0, [[128, 8], [1, 128]]), func=AF.Copy),
                  reads=[tk3], writes=[("yab", 8 + q) for q in range(8)])
            yield
            for gi in range(2):
                p.add("dve", lambda e, gi=gi: e.tensor_tensor(out=xdd[:, gi * 512:(gi + 1) * 512], in0=xdt[:, gi * 512:(gi + 1) * 512],
                                                              in1=sap(seg[gi], 127, [[128, 8], [0, 64]]), op=ALU.mult),
                      reads=["xdt", ("seg", gi)], writes=[("xdd", gi)])
                p.add("pe", lambda e, gi=gi: e.matmul(psC[:, gi * 512:(gi + 1) * 512], lhsT=btok[:, gi, :], rhs=xdd[:, gi * 512:(gi + 1) * 512],
                                                      start=True, stop=True), reads=["btok", ("xdd", gi), ("seg", 0), ("seg", 1), ("eb", 0), ("eb", 1)],
                      writes=["psC"])
                p.add("dve", lambda e, gi=gi: e.tensor_tensor(out=hin[:, gi * 512:(gi + 1) * 512], in0=hin[:, gi * 512:(gi + 1) * 512],
                                                              in1=sap(eb[gi], 127, [[128, 8], [0, 64]]), op=ALU.mult),
                      reads=["hin", ("eb", gi)], writes=["hin"])
            p.add("dve", lambda e: e.tensor_tensor(out=hin[:], in0=hin[:], in1=psC[:], op=ALU.add), reads=["hin", "psC"], writes=["hin"])
            p.add("act", lambda e: e.activation(out=hbf[:], in_=hin[:], func=AF.Copy), reads=["hin"], writes=["hbf"])

        def tile(i):
            io.load(i)
            io.norm(i)
            yT = io.yT[0]
            wi = [0]

            def nextw(c0, ncols):
                par = wi[0] % 2
                wi[0] += 1
                load_w(c0, ncols, par)
                return par
            for half in range(2):
                par = nextw(half * 512, 512)
                for jj in range(4):
                    pt, pk = featmajor_group(yT, par, jj)
                    gi = half * 4 + jj
                    p.add("act", lambda e, gi=gi, pt=pt: e.activation(out=uT[:, gi, :], in_=pt[:], func=AF.Gelu),
                          reads=[pk], writes=[("uT", gi)])
            for which, dstt, fn, nm in ((1, vg, AF.Gelu, "vg"), (2, zs, AF.Silu, "zs")):
                for half in range(2):
                    par = nextw(which * D + half * 512, 512)
                    for s4 in range(4):
                        pt, pk = tokmajor_group(yT, par, s4)
                        p.add("act", lambda e, s4=s4, half=half, pt=pt, dstt=dstt, fn=fn: e.activation(
                            out=dstt[:, s4, half * 512:(half + 1) * 512], in_=pt[:], func=fn), reads=[pk], writes=[(nm, s4)])
            par = nextw(4608, 16)
            for s4 in range(4):
                pt, pk = tokmajor_group(yT, par, s4, ncols=16)
                p.add("dve", lambda e, s4=s4, pt=pt: e.tensor_tensor(out=dtt[:, s4, :], in0=pt[:, 0:16], in1=dtb_bc[:], op=ALU.add),
                      reads=[pk, "dtb_bc"], writes=[("dtt", s4)])
            p.add("act", lambda e: e.activation(out=dtt[:], in_=dtt[:], func=AF.Exp), reads=[("dtt", s4) for s4 in range(4)],
                  writes=[("dtt", s4) for s4 in range(4)])
            p.add("act", lambda e: e.activation(out=dtt[:], in_=dtt[:], func=AF.Ln, bias=1.0), reads=[("dtt", s4) for s4 in range(4)],
                  writes=[("dtt", s4) for s4 in range(4)])
            p.add("dve", lambda e: e.tensor_tensor(out=dA[:], in0=dtt[:], in1=sap(a_bc, 0, [[0, 4], [1, 16]]), op=ALU.mult),
                  reads=[("dtt", s4) for s4 in range(4)] + ["a_bc"], writes=[("dA", s4) for s4 in range(4)])
            for grp in range(3):
                par = nextw(3 * D + grp * 512, 512)
                for jj in range(4):
                    ch = grp * 4 + jj
                    pt, pk = featmajor_group(yT, par, jj)
                    conv_chunk(ch, pt, pk)
            def run_gens(gens):
                gens = list(gens)
                while gens:
                    for gn in list(gens):
                        try:
                            next(gn)
                        except StopIteration:
                            gens.remove(gn)
            run_gens([gmlp_block(0)])
            for s4 in range(4):
                for _ in range(per_blk):
                    if deferred:
                        p.add("pq", deferred.pop(0), writes=[])
                gens = [ssd_chunk(s4, first=(i == 0 and s4 == 0))]
                if s4 + 1 < 4:
                    gens.append(gmlp_block(s4 + 1))
                run_gens(gens)
            for half in range(4):
                p.add("sp", lambda e, half=half: e.dma_start(out=wob[:], in_=dap(g.hy_w_out_bf, half * 256, [[D, 128], [128 * D, 16], [1, 256]])),
                      writes=["wob"])
                for dcl in range(2):
                    dc = half * 2 + dcl
                    pt, pk = P_()
                    for k16 in range(16):
                        p.add("pe", lambda e, k16=k16, dcl=dcl, pt=pt: e.matmul(
                            pt[:], lhsT=wob[:, k16, dcl * 128:(dcl + 1) * 128], rhs=yab[:, k16, :], start=(k16 == 0), stop=(k16 == 15)),
                            reads=["wob", ("yab", k16)], writes=[pk])
                    io.residual(i, dc, pt, pk)
            io.store(i)
            if hasattr(g, "dbgb") and i == 0:
                p.add("sp", lambda e: e.dma_start(out=g.dbgb.ap(), in_=yab[:]), reads=[("yab", k) for k in range(16)])
        for i in range(ntiles):
            tile(i)
        p.emit()


def out_pass(C, g, cs, src, l, ntiles=NT):
    nc = C.nc
    p = Prog(C)
    with contextlib.ExitStack() as es:
        sb = lambda n, s, dt: es.enter_context(nc.sbuf_tensor(C.name(n), list(s), dt))
        ps = lambda n: es.enter_context(nc.psum_tensor(C.name(n), [128, 512], F32))
        io = TileIO(C, p, es, g, cs, "hT", src, None, l, 0)
        ot = [sb("ot", [128, 4, D], F32) for _ in range(2)]
        pst = [ps("pst") for _ in range(2)]
        io.load(0)
        evc = [0]

        def tile(i):
            if i + 1 < ntiles:
                io.load(i + 1)
            b = i % io.nbuf
            hT = io.hT[b]
            io._rstd(b)
            for kc in range(KC):
                gcol = cs.normg[:, (l * 4 + 3) * KC + kc:(l * 4 + 3) * KC + kc + 1]
                p.add("dve", lambda e, kc=kc, gcol=gcol: e.scalar_tensor_tensor(
                    out=hT[:, kc, :], in0=hT[:, kc, :], scalar=gcol, in1=io.rstd[:], op0=ALU.mult, op1=ALU.mult),
                    reads=[("hT", b, kc), "rstd", "normg"], writes=[("hT", b, kc)])
            o = ot[i % 2]
            for s4 in range(4):
                for k2 in range(2):
                    ev = evc[0]
                    evc[0] += 1
                    pt = pst[ev % 2]
                    pk = ("pst", ev % 2)
                    for q in range(4):
                        kc = k2 * 4 + q
                        p.add("pe", lambda e, kc=kc, q=q, s4=s4, pt=pt: e.transpose(
                            pt[:, q * 128:(q + 1) * 128], hT[:, kc, s4 * 128:(s4 + 1) * 128], cs.ident[:]),
                            reads=[("hT", b, kc), "ident"], writes=[pk])
                    if ev % 2 == 0:
                        p.add("act", lambda e, pt=pt, s4=s4, k2=k2: e.activation(
                            out=o[:, s4, k2 * 512:(k2 + 1) * 512], in_=pt[:], func=AF.Copy),
                            reads=[pk], writes=[("ot", i % 2, s4)])
                    else:
                        p.add("dve", lambda e, pt=pt, s4=s4, k2=k2: e.tensor_copy(
                            out=o[:, s4, k2 * 512:(k2 + 1) * 512], in_=pt[:]),
                            reads=[pk], writes=[("ot", i % 2, s4)])
            p.add("aq", lambda e: e.dma_start(out=dap(g.out, i * T * D, [[D, 128], [128 * D, 4], [1, D]]), in_=o[:]),
                  reads=[("ot", i % 2, s4) for s4 in range(4)])
        for i in range(ntiles):
            tile(i)
        p.emit()


FFN_NEED = {"x", "cT", "mod_w0", "mod_bT", "norm_gT", "wg_00", "wu_00", "wd_00"}


def dbg_dump(C, g, cs):
    nc = C.nc
    p = Prog(C)
    p.add("sp", lambda e: e.dma_start(out=dap(g.dbg, 0, [[2048, 128], [1, 144]]), in_=cs.modT[:]))
    p.add("sp", lambda e: e.dma_start(out=dap(g.dbg, 144, [[2048, 128], [1, 48]]), in_=cs.A[:]))
    p.add("sp", lambda e: e.dma_start(out=dap(g.dbg, 192, [[2048, 128], [1, 48]]), in_=cs.gate[:]))
    p.add("sp", lambda e: e.dma_start(out=dap(g.dbg, 240, [[2048, 128], [1, 128]]), in_=cs.ident[:]))
    p.add("sp", lambda e: e.dma_start(out=dap(g.dbg, 368, [[2048, 128], [1, 8]]), in_=cs.condT[:]))
    p.emit()


def build(mode="full", ntiles=NT):
    nc = bass.Bass("TRN2", target_bir_lowering=False)
    need = None
    if mode in ("setup_test", "ffn_test"):
        need = FFN_NEED
    if mode == "hyb_test":
        need = {"hin", "cT", "mod_w0", "mod_bT", "norm_gT", "hy_w_in", "hy_w_out", "gm_v_norm_g", "gm_w_s", "gm_b_s", "conv_wT", "conv_bT",
                "ssd_dt_bias", "ssd_a_log", "ssd_d", "ssd_norm_g"}
    if mode == "attn_test":
        need = {"hin", "cT", "mod_w1", "mod_bT", "norm_gT", "sb_w_qkv", "sb_qgT", "sb_kgT", "sb_w_o"}
    g = declare_io(nc, mode, need)
    with contextlib.ExitStack() as top:
        C = Ctx(nc, top)
        cs = Consts(C, top)
        if mode in ("full", "full_test"):
            nt = ntiles
            setup_pass(C, g, cs, layers=(0, 1), cast=((0, 0),))
            ffn_pass(C, g, cs, 0, 0, "x", g.x, g.hT[0], ntiles=nt, extra_casts=("hy",))
            hybrid_pass(C, g, cs, 0, g.hT[0], g.hT[1], ntiles=nt, extra_casts=((0, 1), (1, 0), "sb", (1, 1)))
            ffn_pass(C, g, cs, 0, 1, "hT", g.hT[1], g.hT[0], ntiles=nt)
            ffn_pass(C, g, cs, 1, 0, "hT", g.hT[0], g.hT[1], prenorm=0, ntiles=nt)
            attn_qkv_pass(C, g, cs, 1, g.hT[1], ntiles=nt)
            attn_core_pass(C, g, cs, ntiles=nt)
            attn_out_pass(C, g, cs, 1, g.hT[1], g.hT[0], ntiles=nt)
            ffn_pass(C, g, cs, 1, 1, "hT", g.hT[0], g.hT[1], ntiles=nt)
            out_pass(C, g, cs, g.hT[1], 1, ntiles=nt)
        if mode == "setup_test":
            setup_pass(C, g, cs, layers=(0,), cast=("ffn",))
            dbg_dump(C, g, cs)
        if mode == "ffn_test":
            setup_pass(C, g, cs, layers=(0,), cast=("ffn",))
            ffn_pass(C, g, cs, 0, 0, "x", g.x, g.hT[0], ntiles=ntiles)
            out_pass(C, g, cs, g.hT[0], 0, ntiles=ntiles)
        if mode == "attn_test":
            setup_pass(C, g, cs, layers=(1,), cast=("sb",))
            attn_qkv_pass(C, g, cs, 1, g.hin, ntiles=ntiles)
            attn_core_pass(C, g, cs, ntiles=ntiles)
            attn_out_pass(C, g, cs, 1, g.hin, g.hT[0], ntiles=ntiles)
        if mode == "hyb_test":
            setup_pass(C, g, cs, layers=(0,), cast=("hy",))
            hybrid_pass(C, g, cs, 0, g.hin, g.hT[0], ntiles=ntiles)
        final_wait(C)
    return nc, g


def make_in_maps(inp):
    f = lambda a: np.ascontiguousarray(np.asarray(a, dtype=np.float32))
    shared = {
        "mod_w0": f(inp["mod_w"][0]), "mod_w1": f(inp["mod_w"][1]),
        "mod_bT": f(np.asarray(inp["mod_b"]).reshape(2, 72, 128).transpose(2, 0, 1).reshape(128, 144)),
        "norm_gT": f(np.asarray(inp["norm_g"]).reshape(2, 4, KC, 128).transpose(3, 0, 1, 2).reshape(128, 64)),
        "hy_w_in": f(inp["hy_w_in"][0]), "hy_w_out": f(inp["hy_w_out"][0]),
        "gm_v_norm_g": f(inp["gm_v_norm_g"]), "gm_w_s": f(inp["gm_w_s"][0]),
        "gm_b_s": f(np.asarray(inp["gm_b_s"]).reshape(1, 1024)),
        "conv_wT": f(np.asarray(inp["ssd_conv_w"][0]).reshape(4, 12, 128).transpose(2, 1, 0).reshape(128, 48)),
        "conv_bT": f(np.asarray(inp["ssd_conv_b"][0]).reshape(12, 128).T),
        "ssd_dt_bias": f(inp["ssd_dt_bias"]), "ssd_a_log": f(inp["ssd_a_log"]), "ssd_d": f(inp["ssd_d"]),
        "ssd_norm_g": f(inp["ssd_norm_g"]),
        "sb_w_qkv": f(inp["sb_w_qkv"][0]),
        "sb_qgT": f(np.asarray(inp["sb_q_norm_g"]).reshape(64, 1)), "sb_kgT": f(np.asarray(inp["sb_k_norm_g"]).reshape(64, 1)),
        "sb_w_o": f(inp["sb_w_o"][0]),
    }
    for l in range(2):
        for ff in range(2):
            shared["wg_%d%d" % (l, ff)] = f(inp["ffn_w_gate"][l, ff])
            shared["wu_%d%d" % (l, ff)] = f(inp["ffn_w_up"][l, ff])
            shared["wd_%d%d" % (l, ff)] = f(inp["ffn_w_down"][l, ff])
    maps = []
    x = np.asarray(inp["x"], dtype=np.float32)
    c = np.asarray(inp["c"], dtype=np.float32)
    for b in range(NCORES):
        m = dict(shared)
        m["x"] = np.ascontiguousarray(x[b])
        m["cT"] = np.ascontiguousarray(c[b].reshape(KC, 128).T)
        maps.append(m)
    return maps


def kernel(**inputs):
    nc, g = build("full")
    in_maps = make_in_maps(inputs)
    res = run_bass_kernel_spmd(nc, in_maps, core_ids=list(range(NCORES)))
    return np.stack([np.asarray(r["out"], dtype=np.float32) for r in res.results], axis=0)
```
